# Optimizing a Trainium2 kernel written in Bass

```python
import math
import jax, jax.numpy as jnp
from jax import lax
import numpy as np

D_MODEL = 1024
BATCH = 16
SEQ = 2048
DEPTH = 1

D_RNN = 1344
LRU_BLOCKS = 4
LRU_BLOCK_W = D_RNN // LRU_BLOCKS
CONV_W = 4
LRU_C = 8.0
N_HEADS = 16
HEAD_DIM = 64
N_KV_GROUPS = 4
HEADS_PER_GROUP = N_HEADS // N_KV_GROUPS
CMP_BLOCK = 32
CMP_STRIDE = 16
SEL_BLOCK = 64
N_SELECT = 16
WINDOW = 512
PHI_HIDDEN = 256
Q_BLOCK = 32
SEL_FORCED = 1e4
NEG_INF = -1e30
REL_BUCKETS = 32
REL_MAX_DIST = 128
D_FF = 4 * D_MODEL
NORM_EPS = 1e-6

Q_W = N_HEADS * HEAD_DIM
KV_W = N_KV_GROUPS * HEAD_DIM
SPLIT_SIZES = (D_RNN, D_RNN, Q_W, KV_W, KV_W, KV_W, KV_W, KV_W, KV_W, 3 * N_HEADS, D_MODEL, D_MODEL)
D_IN = 2 * D_RNN + Q_W + 6 * KV_W + 3 * N_HEADS + 2 * D_MODEL

kernel_name = "hybrid_rglru_nsa_sqrelu"


def rmsnorm(x, g):
    xf = x.astype(jnp.float32)
    y = xf * lax.rsqrt(jnp.mean(xf * xf, axis=-1, keepdims=True) + NORM_EPS)
    return y.astype(x.dtype) * g


def t5_bucket(dist):
    max_exact = REL_BUCKETS // 2
    d = jnp.maximum(dist, 0)
    df = jnp.maximum(d.astype(jnp.float32), 1.0)
    large = max_exact + (jnp.log(df / max_exact) / math.log(REL_MAX_DIST / max_exact)
                         * (REL_BUCKETS - max_exact)).astype(jnp.int32)
    large = jnp.minimum(large, REL_BUCKETS - 1)
    return jnp.where(d < max_exact, d, large)


def masked_softmax(s, mask):
    p = jax.nn.softmax(jnp.where(mask, s.astype(jnp.float32), NEG_INF), axis=-1)
    return jnp.where(mask, p, 0.0)


def causal_depthwise_conv(x, w, b):
    y = lax.conv_general_dilated(x, w[:, None, :], window_strides=(1,), padding=[(CONV_W - 1, 0)],
                                 dimension_numbers=("NWC", "WIO", "NWC"),
                                 feature_group_count=x.shape[-1])
    return y + b


def block_diag_linear(x, w, b):
    xb = x.reshape(x.shape[0], x.shape[1], LRU_BLOCKS, LRU_BLOCK_W)
    y = jnp.einsum("btni,nij->btnj", xb, w) + b
    return y.reshape(x.shape)


def rg_lru(x, w_a, b_a, w_x, b_x, lam):
    r = jax.nn.sigmoid(block_diag_linear(x, w_a, b_a)).astype(jnp.float32)
    i = jax.nn.sigmoid(block_diag_linear(x, w_x, b_x))
    log_a = -LRU_C * r * jax.nn.softplus(-lam.astype(jnp.float32))
    a = jnp.exp(log_a)
    mult = jnp.sqrt(-jnp.expm1(2.0 * log_a))
    first = (jnp.arange(x.shape[1]) == 0)[None, :, None]
    mult = jnp.where(first, 1.0, mult)
    u = mult * (i * x).astype(jnp.float32)

    def combine(c1, c2):
        a1, b1 = c1
        a2, b2 = c2
        return a1 * a2, a2 * b1 + b2

    _, h = lax.associative_scan(combine, (a, u), axis=1)
    return h.astype(x.dtype)


def compress_blocks(z, pe, w1, w2):
    b, t = z.shape[0], z.shape[1]
    n_c = (t - CMP_BLOCK) // CMP_STRIDE + 1
    idx = jnp.arange(n_c)[:, None] * CMP_STRIDE + jnp.arange(CMP_BLOCK)[None, :]
    blk = z[:, idx] + pe[None, None, :, None, :]
    flat = blk.transpose(0, 3, 1, 2, 4).reshape(b, N_KV_GROUPS, n_c, CMP_BLOCK * HEAD_DIM)
    return jax.nn.gelu(flat @ w1, approximate=True) @ w2


def nsa_mixer(q, k_c, v_c, k_s, v_s, k_w, v_w, g_nsa, rel_bias,
              phi_k_pe, phi_k_w1, phi_k_w2, phi_v_pe, phi_v_w1, phi_v_w2,
              q_norm, kc_norm, ks_norm, kw_norm):
    B, T, _ = q.shape
    G, R, hd = N_KV_GROUPS, HEADS_PER_GROUP, HEAD_DIM
    qh = rmsnorm(q.reshape(B, T, N_HEADS, hd), q_norm) * (hd ** -0.5)
    qh = qh.reshape(B, T, G, R, hd).transpose(0, 2, 3, 1, 4)
    kv = lambda z: z.reshape(B, T, G, hd)
    kc = rmsnorm(compress_blocks(kv(k_c), phi_k_pe, phi_k_w1, phi_k_w2), kc_norm)
    vc = compress_blocks(kv(v_c), phi_v_pe, phi_v_w1, phi_v_w2)
    n_c = kc.shape[2]
    n_sblk = T // SEL_BLOCK
    ks = rmsnorm(kv(k_s), ks_norm).transpose(0, 2, 1, 3).reshape(B, G, n_sblk, SEL_BLOCK, hd)
    vs = kv(v_s).transpose(0, 2, 1, 3).reshape(B, G, n_sblk, SEL_BLOCK, hd)
    pad = ((0, 0), (0, 0), (WINDOW, 0), (0, 0))
    kw = jnp.pad(rmsnorm(kv(k_w), kw_norm).transpose(0, 2, 1, 3), pad)
    vw = jnp.pad(kv(v_w).transpose(0, 2, 1, 3), pad)
    gates = jax.nn.sigmoid(g_nsa.reshape(B, T, G, R, 3).transpose(0, 2, 3, 1, 4))
    bias_gr = rel_bias.T.reshape(G, R, REL_BUCKETS)

    cstart = jnp.arange(n_c) * CMP_STRIDE
    cend = cstart + CMP_BLOCK - 1
    sj = jnp.arange(n_sblk)
    cover = ((cstart[:, None] < (sj[None, :] + 1) * SEL_BLOCK)
             & (cend[:, None] >= sj[None, :] * SEL_BLOCK)).astype(jnp.float32)
    n_top = min(N_SELECT, n_sblk)
    L = n_top * SEL_BLOCK
    bi = jnp.arange(B)[:, None, None, None]
    gi = jnp.arange(G)[None, :, None, None]
    gi5 = jnp.arange(G)[None, :, None, None, None]
    ri5 = jnp.arange(R)[None, None, :, None, None]

    def attend_block(blk):
        t0 = blk * Q_BLOCK
        tq = t0 + jnp.arange(Q_BLOCK)
        qb = lax.dynamic_slice_in_dim(qh, t0, Q_BLOCK, axis=3)
        gb = lax.dynamic_slice_in_dim(gates, t0, Q_BLOCK, axis=3)
        s_c = (jnp.einsum("bgrqd,bgcd->bgrqc", qb, kc).astype(jnp.float32)
               + bias_gr[:, :, t5_bucket(tq[:, None] - cend[None, :])])
        p_c = masked_softmax(s_c, cend[None, :] <= tq[:, None])
        o_c = jnp.einsum("bgrqc,bgcd->bgrqd", p_c.astype(vc.dtype), vc)
        imp = jnp.einsum("bgrqc,cj->bgqj", p_c, cover)
        qblk = tq // SEL_BLOCK
        causal_j = sj[None, :] <= qblk[:, None]
        forced = causal_j & ((sj[None, :] == 0) | (sj[None, :] >= qblk[:, None] - 1))
        score = jnp.where(forced, SEL_FORCED, jnp.where(causal_j, imp, -1.0))
        top_v, top_i = lax.top_k(score, n_top)
        k_sel = ks[bi, gi, top_i].reshape(B, G, Q_BLOCK, L, hd)
        v_sel = vs[bi, gi, top_i].reshape(B, G, Q_BLOCK, L, hd)
        pos = top_i[..., None] * SEL_BLOCK + jnp.arange(SEL_BLOCK)
        mask_s = ((top_v >= 0.0)[..., None] & (pos <= tq[:, None, None])).reshape(B, G, Q_BLOCK, L)
        dist_s = tq[:, None] - pos.reshape(B, G, Q_BLOCK, L)
        s_s = (jnp.einsum("bgrqd,bgqld->bgrql", qb, k_sel).astype(jnp.float32)
               + bias_gr[gi5, ri5, t5_bucket(dist_s)[:, :, None]])
        p_s = masked_softmax(s_s, mask_s[:, :, None])
        o_s = jnp.einsum("bgrql,bgqld->bgrqd", p_s.astype(v_sel.dtype), v_sel)
        kwb = lax.dynamic_slice_in_dim(kw, t0, WINDOW + Q_BLOCK, axis=2)
        vwb = lax.dynamic_slice_in_dim(vw, t0, WINDOW + Q_BLOCK, axis=2)
        sk = t0 - WINDOW + jnp.arange(WINDOW + Q_BLOCK)
        dist_w = tq[:, None] - sk[None, :]
        mask_w = (dist_w >= 0) & (dist_w < WINDOW) & (sk[None, :] >= 0)
        s_w = (jnp.einsum("bgrqd,bgsd->bgrqs", qb, kwb).astype(jnp.float32)
               + bias_gr[:, :, t5_bucket(dist_w)])
        p_w = masked_softmax(s_w, mask_w)
        o_w = jnp.einsum("bgrqs,bgsd->bgrqd", p_w.astype(vwb.dtype), vwb)
        return gb[..., 0:1] * o_c + gb[..., 1:2] * o_s + gb[..., 2:3] * o_w

    o = lax.map(attend_block, jnp.arange(T // Q_BLOCK))
    return o.transpose(1, 0, 4, 2, 3, 5).reshape(B, T, Q_W)


def hybrid_layer(x, rel_bias, norm_mix, w_in, conv_w, conv_b, gate_a_w, gate_a_b, gate_x_w, gate_x_b,
                 lru_lambda, phi_k_pe, phi_k_w1, phi_k_w2, phi_v_pe, phi_v_w1, phi_v_w2,
                 q_norm, kc_norm, ks_norm, kw_norm, proj_a, proj_b, w_out,
                 norm_mlp, w_mlp_in, w_mlp_out):
    xn = rmsnorm(x, norm_mix)
    cuts = [int(c) for c in np.cumsum(SPLIT_SIZES)[:-1]]
    (u_rnn, u_gate, q, k_c, v_c, k_s, v_s, k_w, v_w, g_nsa, g_a, g_b) = jnp.split(xn @ w_in, cuts, axis=-1)
    h_a = rg_lru(causal_depthwise_conv(u_rnn, conv_w, conv_b), gate_a_w, gate_a_b, gate_x_w, gate_x_b, lru_lambda)
    y_a = h_a * jax.nn.gelu(u_gate, approximate=True)
    y_b = nsa_mixer(q, k_c, v_c, k_s, v_s, k_w, v_w, g_nsa, rel_bias,
                    phi_k_pe, phi_k_w1, phi_k_w2, phi_v_pe, phi_v_w1, phi_v_w2,
                    q_norm, kc_norm, ks_norm, kw_norm)
    merged = jax.nn.sigmoid(g_a) * (y_a @ proj_a) + jax.nn.sigmoid(g_b) * (y_b @ proj_b)
    h = x + merged @ w_out
    z = rmsnorm(h, norm_mlp) @ w_mlp_in
    return h + jnp.square(jax.nn.relu(z)) @ w_mlp_out


def setup_inputs(seed: int = 0) -> dict:
    key = jax.random.key(seed)
    ks = jax.random.split(key, 32)
    nrm = lambda k, shape, scale: jax.random.normal(k, shape, jnp.float32) * scale
    gain = lambda k, shape: 1.0 + 0.05 * jax.random.normal(k, shape, jnp.float32)
    a0 = jax.random.uniform(ks[9], (DEPTH, D_RNN), jnp.float32, minval=0.9, maxval=0.999)
    s = a0 ** (1.0 / LRU_C)
    lam = jnp.log(s) - jnp.log1p(-s)
    cw = CMP_BLOCK * HEAD_DIM
    return {
        "x": nrm(ks[0], (BATCH, SEQ, D_MODEL), 1.0),
        "norm_mix": gain(ks[1], (DEPTH, D_MODEL)),
        "w_in": nrm(ks[2], (DEPTH, D_MODEL, D_IN), D_MODEL ** -0.5),
        "conv_w": nrm(ks[3], (DEPTH, CONV_W, D_RNN), CONV_W ** -0.5),
        "conv_b": nrm(ks[4], (DEPTH, D_RNN), 0.1),
        "gate_a_w": nrm(ks[5], (DEPTH, LRU_BLOCKS, LRU_BLOCK_W, LRU_BLOCK_W), LRU_BLOCK_W ** -0.5),
        "gate_a_b": nrm(ks[6], (DEPTH, LRU_BLOCKS, LRU_BLOCK_W), 0.1),
        "gate_x_w": nrm(ks[7], (DEPTH, LRU_BLOCKS, LRU_BLOCK_W, LRU_BLOCK_W), LRU_BLOCK_W ** -0.5),
        "gate_x_b": nrm(ks[8], (DEPTH, LRU_BLOCKS, LRU_BLOCK_W), 0.1),
        "lru_lambda": lam,
        "phi_k_pe": nrm(ks[10], (DEPTH, CMP_BLOCK, HEAD_DIM), 0.1),
        "phi_k_w1": nrm(ks[11], (DEPTH, cw, PHI_HIDDEN), cw ** -0.5),
        "phi_k_w2": nrm(ks[12], (DEPTH, PHI_HIDDEN, HEAD_DIM), PHI_HIDDEN ** -0.5),
        "phi_v_pe": nrm(ks[13], (DEPTH, CMP_BLOCK, HEAD_DIM), 0.1),
        "phi_v_w1": nrm(ks[14], (DEPTH, cw, PHI_HIDDEN), cw ** -0.5),
        "phi_v_w2": nrm(ks[15], (DEPTH, PHI_HIDDEN, HEAD_DIM), PHI_HIDDEN ** -0.5),
        "q_norm": gain(ks[16], (DEPTH, HEAD_DIM)),
        "kc_norm": gain(ks[17], (DEPTH, HEAD_DIM)),
        "ks_norm": gain(ks[18], (DEPTH, HEAD_DIM)),
        "kw_norm": gain(ks[19], (DEPTH, HEAD_DIM)),
        "rel_bias": nrm(ks[20], (REL_BUCKETS, N_HEADS), 0.5),
        "proj_a": nrm(ks[21], (DEPTH, D_RNN, D_MODEL), D_RNN ** -0.5),
        "proj_b": nrm(ks[22], (DEPTH, Q_W, D_MODEL), Q_W ** -0.5),
        "w_out": nrm(ks[23], (DEPTH, D_MODEL, D_MODEL), D_MODEL ** -0.5),
        "norm_mlp": gain(ks[24], (DEPTH, D_MODEL)),
        "w_mlp_in": nrm(ks[25], (DEPTH, D_MODEL, D_FF), D_MODEL ** -0.5),
        "w_mlp_out": nrm(ks[26], (DEPTH, D_FF, D_MODEL), D_FF ** -0.5),
    }


def reference(x, norm_mix, w_in, conv_w, conv_b, gate_a_w, gate_a_b, gate_x_w, gate_x_b, lru_lambda,
              phi_k_pe, phi_k_w1, phi_k_w2, phi_v_pe, phi_v_w1, phi_v_w2,
              q_norm, kc_norm, ks_norm, kw_norm, rel_bias, proj_a, proj_b, w_out,
              norm_mlp, w_mlp_in, w_mlp_out):
    h = x
    for l in range(DEPTH):
        h = hybrid_layer(h, rel_bias, norm_mix[l], w_in[l], conv_w[l], conv_b[l],
                         gate_a_w[l], gate_a_b[l], gate_x_w[l], gate_x_b[l], lru_lambda[l],
                         phi_k_pe[l], phi_k_w1[l], phi_k_w2[l], phi_v_pe[l], phi_v_w1[l], phi_v_w2[l],
                         q_norm[l], kc_norm[l], ks_norm[l], kw_norm[l],
                         proj_a[l], proj_b[l], w_out[l], norm_mlp[l], w_mlp_in[l], w_mlp_out[l])
    return h
```

```python
import os
import math
import numpy as np
import concourse.bass as bass
import concourse.mybir as mybir
from concourse.bass_utils import run_bass_kernel_spmd
from contextlib import ExitStack

F32 = mybir.dt.float32
BF16 = mybir.dt.bfloat16
AF = mybir.ActivationFunctionType
ALU = mybir.AluOpType
AX = mybir.AxisListType

T = 2048
D = 1024
NT = 16
D_RNN = 1344
D_IN = 7344
D_FF = 4096
EPS = 1e-6
NEG = -30000.0
OFF = 160
C_URNN, C_UGATE, C_Q, C_KC, C_VC, C_KS, C_VS, C_KW, C_VW, C_GN, C_GA, C_GB = (
    0, 1344, 2688, 3712, 3968, 4224, 4480, 4736, 4992, 5248, 5296, 6320)


class Buf:
    __slots__ = ("name", "w", "r", "dsem", "dkey", "dcnt", "excl")

    def __init__(self, name):
        self.name = name
        self.excl = False
        self.w = None
        self.r = []
        self.dsem = None
        self.dkey = None
        self.dcnt = 0


class Sched:
    def __init__(self, nc, es):
        self.nc = nc
        self.es = es
        self.engs = {"pe": nc.tensor, "dve": nc.vector, "act": nc.scalar, "pool": nc.gpsimd, "sp": nc.sync}
        self.sem = {k: es.enter_context(nc.semaphore("s_" + k)) for k in self.engs}
        self.cnt = {k: 0 for k in self.engs}
        self.waited = {k: {} for k in self.engs}
        self.nbuf = 0
        self.ninst = 0
        self.dbufs = []
        self.bufs = []
        self.free_dsems = {"sw": [], "hw": []}
        self.ndsem = 0

    def buf(self, name=None):
        self.nbuf += 1
        b = Buf(f"{name or 'b'}{self.nbuf}")
        self.bufs.append(b)
        return b

    def _need(self, eng, dep):
        key, sem, cnt = dep
        if key == eng and eng == "pe":
            return
        w = self.waited[eng]
        if w.get(key, 0) >= cnt:
            return
        self.engs[eng].wait_ge(sem, cnt)
        w[key] = cnt
        self.ninst += 1

    def _deps(self, eng, reads, writes):
        for b in reads:
            if b.w is not None:
                self._need(eng, b.w)
            if b.excl:
                for d in b.r:
                    if d[0] != eng:
                        self._need(eng, d)
        for b in writes:
            if b.w is not None:
                self._need(eng, b.w)
            for d in b.r:
                self._need(eng, d)

    def _mark(self, me, reads, writes):
        for b in writes:
            b.w = me
            b.r = []
        for b in reads:
            if b in writes:
                continue
            b.r.append(me)
            if len(b.r) > 10:
                d = {}
                for k, s, v in b.r:
                    if k not in d or d[k][2] < v:
                        d[k] = (k, s, v)
                b.r = list(d.values())

    def op(self, eng, fn, reads=(), writes=()):
        self._deps(eng, reads, writes)
        inst = fn(self.engs[eng])
        self.cnt[eng] += 1
        inst.then_inc(self.sem[eng], 1)
        self.ninst += 1
        self._mark((eng, self.sem[eng], self.cnt[eng]), reads, writes)
        return inst

    def dma(self, q, out, in_, reads=(), writes=(), own=None, **kw):
        if own is None:
            own = writes[0] if writes else reads[0]
        cls = "sw" if q == "pool" else "hw"
        if own.dsem is None:
            if self.free_dsems[cls]:
                own.dsem, own.dkey, own.dcnt = self.free_dsems[cls].pop()
            else:
                self.ndsem += 1
                own.dkey = f"dsem{cls}{self.ndsem}"
                own.dsem = self.es.enter_context(self.nc.semaphore(own.dkey))
                own.dcnt = 0
            self.dbufs.append(own)
        assert own.dkey.startswith("dsem" + cls), (own.name, own.dkey, q)
        for b in reads:
            if b.w is not None and b.w[0] != own.dkey:
                self._need(q, b.w)
        for b in writes:
            if b.w is not None and b.w[0] != own.dkey:
                self._need(q, b.w)
            for d in b.r:
                if d[0] != own.dkey:
                    self._need(q, d)
        inst = self.engs[q].dma_start(out=out, in_=in_, **kw)
        own.dcnt += 16
        inst.then_inc(own.dsem, 16)
        self.ninst += 1
        self._mark((own.dkey, own.dsem, own.dcnt), reads, writes)
        return inst

    def barrier(self, engines=None):
        full = engines is None
        for e in (engines or list(self.engs)):
            for o in self.engs:
                if o != e and self.cnt[o] > 0:
                    self._need(e, (o, self.sem[o], self.cnt[o]))
            if self.cnt[e] > 0 and e != "pe":
                w = self.waited[e]
                if w.get(e, 0) < self.cnt[e]:
                    self.engs[e].wait_ge(self.sem[e], self.cnt[e])
                    w[e] = self.cnt[e]
            for b in self.dbufs:
                if b.dcnt > 0:
                    self._need(e, (b.dkey, b.dsem, b.dcnt))
        if full:
            for b in self.dbufs:
                self.free_dsems["sw" if b.dkey.startswith("dsemsw") else "hw"].append((b.dsem, b.dkey, b.dcnt))
                b.dsem = None
                b.dkey = None
            self.dbufs = []
            for b in self.bufs:
                b.w = None
                b.r = []


class Ring:
    def __init__(self, items):
        self.items = items
        self.i = 0

    def next(self):
        it = self.items[self.i % len(self.items)]
        self.i += 1
        return it


def _t5_bucket(d):
    if d < 16:
        return d
    v = 16 + int(np.float32(np.log(np.float32(d) / np.float32(16.0))) / np.float32(math.log(8.0)) * np.float32(16.0))
    return min(v, 31)


def _bucket_table(n):
    d = np.arange(n)
    df = np.maximum(d.astype(np.float32), np.float32(1.0))
    large = 16 + (np.log(df / np.float32(16.0)) / np.float32(math.log(128 / 16)) * np.float32(16.0)).astype(np.int32)
    large = np.minimum(large, 31)
    return np.where(d < 16, d, large)


def host_consts():
    c = {}
    c["ident"] = np.eye(128, dtype=np.float32)
    bt = _bucket_table(512)
    oh2 = np.zeros((33, 512), np.float32)
    for i in range(512):
        dist = i - OFF
        if dist < 0:
            oh2[32, i] = NEG
        else:
            oh2[bt[dist], i] += 1.0
            oh2[31, i] -= 1.0
    c["oh2"] = oh2
    b4 = np.zeros((128, 4, 128), np.float32)
    kl = np.arange(128)[:, None]
    ql = np.arange(128)[None, :]
    b4[:] = np.where(ql < kl, 0.0, NEG)[:, None, :]
    c["b4"] = b4.reshape(128, 512)
    selc = np.zeros((17, 16, 128), np.float32)
    for qt in range(16):
        for cc in range(128):
            u = cc - 8 * qt
            if -8 <= u < 8:
                selc[u + 8, qt, cc] = 1.0
            elif u >= 8:
                selc[16, qt, cc] = 1.0
    c["selc"] = selc
    wtf = np.full((1, 2048), NEG, np.float32)
    c["wtf"] = wtf
    cs = np.arange(127) * 16
    ce = cs + 31
    sj = np.arange(32)
    cover = ((cs[:, None] < (sj[None, :] + 1) * 64) & (ce[:, None] >= sj[None, :] * 64)).astype(np.float32)
    c["cover"] = cover
    caus = np.zeros((128, 16, 32), np.float32)
    addc = np.zeros((128, 16, 32), np.float32)
    for qt in range(16):
        for p in range(128):
            qb = (qt * 128 + p) // 64
            for j in range(32):
                cz = j <= qb
                forced = cz and (j == 0 or j >= qb - 1)
                caus[p, qt, j] = 1.0 if cz else 0.0
                addc[p, qt, j] = 1e4 if forced else (0.0 if cz else -1.0)
    c["caus"] = caus
    c["addc"] = addc
    eb = np.zeros((32, 2048), np.float32)
    for j in range(32):
        eb[j, j * 64:(j + 1) * 64] = 1.0
    c["eblk"] = eb
    return c


DBG = {}


def build_program(nseq=2, debug=False, stop_after=None):
    nc = bass.Bass("TRN2", target_bir_lowering=False)
    dram = {}

    def din(name, shape, dt=F32):
        dram[name] = nc.dram_tensor(name, list(shape), dt, kind="ExternalInput").ap()
        return dram[name]

    x = din("x", [nseq, T, D])
    w_in = din("w_in", [D, D_IN])
    chtab_d = din("chtab", [112, 12, 8])
    gaw = din("gate_a_w", [4, 336, 336])
    gxw = din("gate_x_w", [4, 336, 336])
    w1kv_d = din("w1kv", [128, 32, 256])
    peT_d = din("peT", [128, 32])
    w2kv_d = din("w2kv", [128, 2, 2, 64])
    gains_d = din("gains", [64, 4])
    relb_d = din("rel_bias", [32, 16])
    proj_a = din("proj_a", [D_RNN, D])
    proj_b = din("proj_b", [D, D])
    w_out = din("w_out", [D, D])
    w_mi = din("w_mlp_in", [D, D_FF])
    w_mo = din("w_mlp_out", [D_FF, D])
    nmix_d = din("norm_mix", [1, D])
    nmlp_d = din("norm_mlp", [1, D])
    ident_d = din("ident", [128, 128])
    oh2_d = din("oh2", [33, 512])
    b4_d = din("b4", [128, 512])
    selc_d = din("selc", [17, 16, 128])
    wtf_d = din("wtf", [1, 2048])
    cover_d = din("cover", [127, 32])
    caus_d = din("caus", [128, 16, 32])
    addc_d = din("addc", [128, 16, 32])
    eblk_d = din("eblk", [32, 2048])
    out = nc.dram_tensor("out", [nseq, T, D], F32, kind="ExternalOutput").ap()
    mscr = nc.dram_tensor("mscr", [16, 128, 512], BF16, kind="Internal").ap()
    dbg = {}
    if debug:
        for nm, shp in (("d_xnT", [128, 8, T]), ("d_ybT", [128, 8, T]), ("d_yaT", [112, 12, T]),
                        ("d_mgT", [128, 8, T]), ("d_paT", [128, 8, T])):
            dbg[nm] = nc.dram_tensor(nm, shp, BF16, kind="ExternalOutput").ap()

    with ExitStack() as es:
        S = Sched(nc, es)

        ARENA_BYTES = 212480
        arena = es.enter_context(nc.sbuf_tensor("arena", [128, ARENA_BYTES // 2], BF16))
        free_list = [[0, ARENA_BYTES]]
        peak = [0]

        def a_alloc(nb, top=False):
            nb = (nb + 63) // 64 * 64
            if top:
                for i in range(len(free_list) - 1, -1, -1):
                    o, sz = free_list[i]
                    if sz >= nb:
                        if sz == nb:
                            free_list.pop(i)
                        else:
                            free_list[i] = [o, sz - nb]
                        peak[0] = max(peak[0], o + sz)
                        return o + sz - nb, nb
                raise RuntimeError(f"arena out of memory (top): need {nb}, free {free_list}")
            for i, (o, sz) in enumerate(free_list):
                if sz >= nb:
                    if sz == nb:
                        free_list.pop(i)
                    else:
                        free_list[i] = [o + nb, sz - nb]
                    peak[0] = max(peak[0], o + nb)
                    return o, nb
            raise RuntimeError(f"arena out of memory: need {nb}, free {free_list}")

        def a_free(o, nb):
            free_list.append([o, nb])
            free_list.sort()
            i = 0
            while i + 1 < len(free_list):
                if free_list[i][0] + free_list[i][1] == free_list[i + 1][0]:
                    free_list[i][1] += free_list[i + 1][1]
                    free_list.pop(i + 1)
                else:
                    i += 1

        def sb(st, name, shape, dt, top=False):
            shape = list(shape)
            esz = 4 if dt == F32 else 2
            n = 1
            for d_ in shape[1:]:
                n *= d_
            o, nb = a_alloc(n * esz, top)
            st.callback(a_free, o, nb)
            v = arena[0:shape[0], o // 2:o // 2 + n * esz // 2]
            if dt == F32:
                v = v.bitcast(F32)
            if len(shape) == 3:
                v = v.rearrange("p (a b) -> p a b", a=shape[1])
            elif len(shape) == 4:
                v = v.rearrange("p (a b c) -> p a b c", a=shape[1], b=shape[2])
            return v

        PS = [es.enter_context(nc.psum_tensor(f"ps{i}", [128, 512], F32)) for i in range(8)]
        bPS = [S.buf(f"ps{i}") for i in range(8)]
        for b_ in bPS:
            b_.excl = True

        def psb(i):
            return PS[i][:, :].bitcast(BF16)

        ident = sb(es, "ident", [128, 128], BF16); b_ident = S.buf("ident")
        chtab = sb(es, "chtab_s", [112, 12, 8], F32); b_chtab = S.buf("chtab")
        cvec = sb(es, "cvec", [112, 12, 2], F32); b_cvec = S.buf("cvec")
        hbias = sb(es, "hbias", [112, 12, 2], F32); b_hbias = S.buf("hbias")
        gains = sb(es, "gains_s", [64, 4], F32); b_gains = S.buf("gains")
        stat = sb(es, "stat", [128, 192], F32)
        stat_ring = Ring([(stat[:, i * 8:(i + 1) * 8], S.buf("stat")) for i in range(24)])

        S.dma("pool", ident[:], ident_d, writes=[b_ident])
        S.dma("sp", chtab[:], chtab_d, writes=[b_chtab])
        S.dma("sp", gains[:], gains_d, writes=[b_gains])
        S.op("act", lambda e: e.activation(out=cvec[:, :, 0], in_=chtab[:, :, 7], func=AF.Exp, scale=-1.0),
             reads=[b_chtab], writes=[b_cvec])
        S.op("act", lambda e: e.activation(out=cvec[:, :, 0], in_=cvec[:, :, 0], func=AF.Ln, bias=1.0),
             reads=[b_cvec], writes=[b_cvec])
        S.op("dve", lambda e: e.tensor_scalar(out=cvec[:, :, 1], in0=cvec[:, :, 0], scalar1=-4.0, scalar2=None,
                                              op0=ALU.mult), reads=[b_cvec], writes=[b_cvec])
        S.op("dve", lambda e: e.tensor_scalar(out=hbias[:], in0=chtab[:, :, 5:7], scalar1=0.5, scalar2=None,
                                              op0=ALU.mult), reads=[b_chtab], writes=[b_hbias])
        S.op("dve", lambda e: e.tensor_scalar(out=cvec[:, :, 0], in0=cvec[:, :, 0], scalar1=-8.0, scalar2=None,
                                              op0=ALU.mult), reads=[b_cvec], writes=[b_cvec])

        with ExitStack() as st0:
            relb = sb(st0, "relb", [33, 16], F32); b_relb = S.buf("relb")
            rbrep = sb(st0, "rbrep", [33, 16, 128], F32); b_rbrep = S.buf("rbrep")
            oh2 = sb(st0, "oh2_s", [33, 512], F32); b_oh2 = S.buf("oh2")
            mt = sb(st0, "mt", [128, 16, 512], BF16); b_mt = S.buf("mt")
            b_mscr = S.buf("mscr")
            S.op("dve", lambda e: e.memset(relb[:], 1.0), writes=[b_relb])
            S.dma("sp", relb[0:32, :], relb_d, writes=[b_relb])
            S.dma("sp", oh2[:], oh2_d, writes=[b_oh2])
            S.op("dve", lambda e: e.tensor_copy(out=rbrep[:], in_=relb[:, :].unsqueeze(2).to_broadcast([33, 16, 128])),
                 reads=[b_relb], writes=[b_rbrep])
            for h in range(16):
                pi = h % 2
                S.op("pe", lambda e: e.matmul(PS[pi][:, :], lhsT=rbrep[:, h, :], rhs=oh2[:, :], start=True, stop=True),
                     reads=[b_rbrep, b_oh2], writes=[bPS[pi]])
                S.op("act", lambda e: e.copy(out=mt[:, h, :], in_=PS[pi][:, :]), reads=[bPS[pi]], writes=[b_mt])
            S.dma("sp", mscr.rearrange("h p f -> p h f"), mt[:], reads=[b_mt], writes=[b_mscr], own=b_mscr)
            S.barrier()

        for s in range(nseq):
            with ExitStack() as sq:
                sx = ExitStack()
                xnT = sb(sx, f"xnT{s}", [128, 8, T], BF16)
                b_xnT = [S.buf("xnT") for _ in range(NT)]

                def proj_fm(pi, wtile, wcols, M, t4, bw):
                    for kc in range(8):
                        S.op("pe", lambda e: e.matmul(PS[pi][0:M, :], lhsT=wtile[:, kc, wcols],
                                                      rhs=xnT[:, kc, t4 * 512:(t4 + 1) * 512],
                                                      start=(kc == 0), stop=(kc == 7)),
                             reads=[bw] + b_xnT[t4 * 4:(t4 + 1) * 4], writes=[bPS[pi]])

                sy = ExitStack()
                ybT = sb(sy, f"ybT{s}", [128, 8, T], BF16)
                b_ybT = [S.buf("ybT") for _ in range(NT)]

                with ExitStack() as st:
                    w1kv = sb(st, "w1kv_s", [128, 32, 256], BF16); b_w1kv = S.buf("w1kv")
                    w2kv = sb(st, "w2kv_s", [128, 2, 2, 64], BF16); b_w2kv = S.buf("w2kv")
                    peT = sb(st, "peT_s", [128, 32], BF16); b_peT = S.buf("peT")
                    cbias = sb(st, "cbias", [128, 4], F32); b_cbias = S.buf("cbias")
                    B0 = sb(st, "B0", [128, 4, 512], BF16); b_B0 = S.buf("B0")
                    B1 = sb(st, "B1", [128, 4, 512], BF16); b_B1 = S.buf("B1")
                    B4 = sb(st, "B4", [128, 512], BF16); b_B4 = S.buf("B4")
                    WTn = sb(st, "WTn", [17, 4, 512], BF16); b_WTn = S.buf("WTn")
                    SelC = sb(st, "SelC", [17, 16, 128], BF16); b_SelC = S.buf("SelC")
                    caus = sb(st, "caus_s", [128, 16, 32], F32); b_caus = S.buf("caus")
                    addc = sb(st, "addc_s", [128, 16, 32], F32); b_addc = S.buf("addc")
                    VCaug = sb(st, "VCaug", [128, 97], BF16); b_VC = S.buf("VCaug")
                    KcT = sb(st, "KcT", [64, 128], BF16); b_KcT = S.buf("KcT")
                    KsT = sb(st, "KsT", [96, T], BF16); b_KsT = [S.buf("KsT") for _ in range(NT)]
                    KwT = sb(st, "KwT", [64, T], BF16); b_KwT = [S.buf("KwT") for _ in range(NT)]
                    Qaug = sb(st, "Qaug", [96, NT, 4, 128], BF16); b_Q = [S.buf("Q") for _ in range(NT)]
                    Vsw = sb(st, "Vsw", [128, 2, NT, 65], BF16); b_V = [S.buf("V") for _ in range(NT)]
                    GATES = sb(st, "GATES", [128, NT, 48], F32); b_G = [S.buf("G") for _ in range(NT)]
                    KVcT = sb(st, "KVcT", [128, 16, 130], BF16); b_KVc = S.buf("KVcT")
                    HT = sb(st, "HT", [128, 4, 128], BF16); b_HT = S.buf("HT")
                    NWs = [sb(st, f"NW{i}", [128, 96], BF16) for i in range(3)]
                    nw_ring = Ring([(NWs[i], S.buf("NW")) for i in range(3)])
                    Wg = [sb(st, f"Wg{i}", [128, 8, 512], BF16) for i in range(2)]
                    wg_ring = Ring([(Wg[i], S.buf("Wg")) for i in range(2)])
                    Wc = [sb(st, f"Wc{i}", [128, 8, 128], BF16) for i in range(2)]
                    wc_ring = Ring([(Wc[i], S.buf("Wc")) for i in range(2)])
                    Wgn = sb(st, "Wgn", [128, 8, 48], BF16); b_Wgn = S.buf("Wgn")
                    sqt = [sb(st, f"sqt{i}", [128, 384], F32) for i in range(2)]
                    sq_ring = Ring([(sqt[i], S.buf("sqt")) for i in range(2)])
                    nrm = [sb(st, f"nrm{i}", [128, 384], BF16) for i in range(3)]
                    nrm_ring = Ring([(nrm[i], S.buf("nrm")) for i in range(3)])
                    Pt = [sb(st, f"Pt{i}", [128, 512], BF16) for i in range(4)]
                    p_ring = Ring([(Pt[i], S.buf("Pt")) for i in range(4)])
                    Yt = [sb(st, f"Yt{i}", [128, 4, 64], F32) for i in range(2)]
                    y_ring = Ring([(Yt[i], [S.buf("Yt") for _ in range(4)]) for i in range(2)])
                    Ytmp = [sb(st, f"Ytmp{i}", [128, 4, 64], F32) for i in range(2)]
                    ytmp_ring = Ring([(Ytmp[i], S.buf("Ytmp")) for i in range(2)])
                    Yb = [sb(st, f"Yb{i}", [128, 256], BF16) for i in range(2)]
                    yb_ring = Ring([(Yb[i], S.buf("Yb")) for i in range(2)])
                    sm = sb(st, "sm", [128, 16, 64], F32)
                    sm_ring = Ring([(sm[:, i, :], S.buf("sm")) for i in range(16)])
                    kcn = sb(st, "kcn", [128, 64], BF16); b_kcn = S.buf("kcn")
                    b_mscr = S.buf("mscr_r")

                    S.dma("pool", w1kv[:, 0:16, :], w1kv_d[:, 0:16, :], writes=[b_w1kv])
                    S.dma("pool", w1kv[:, 16:32, :], w1kv_d[:, 16:32, :], writes=[b_w1kv])
                    S.dma("pool", w2kv[:], w2kv_d, writes=[b_w2kv])
                    S.dma("pool", peT[:], peT_d, writes=[b_peT])
                    S.dma("pool", B4[:], b4_d, writes=[b_B4])
                    S.dma("pool", SelC[:], selc_d, writes=[b_SelC])
                    b_WTf = S.buf("WTf")
                    S.dma("pool", WTn[16:17, :, :], wtf_d.rearrange("o (a b) -> o a b", a=4), writes=[b_WTf])
                    S.op("pool", lambda e: e.memset(Qaug[:], 0.0), writes=b_Q)
                    S.dma("pool", KsT[64:96, :], eblk_d, writes=b_KsT)
                    S.op("pool", lambda e: e.memset(Vsw[:, :, :, 64:65], 1.0), writes=b_V)
                    S.op("pool", lambda e: e.memset(VCaug[:], 0.0), writes=[b_VC])
                    S.op("pool", lambda e: e.memset(VCaug[:, 64:65], 1.0), writes=[b_VC])
                    S.op("pool", lambda e: e.memset(KVcT[:], 0.0), writes=[b_KVc])
                    S.dma("pool", VCaug[0:127, 65:97], cover_d, writes=[b_VC])
                    for nw_, bnw_ in nw_ring.items:
                        S.op("pool", lambda e: e.memset(nw_[:], 0.0), writes=[bnw_])
                    S.op("pool", lambda e: e.memset(HT[:], 0.0), writes=[b_HT])
                    with ExitStack() as st1:
                        gmix = sb(st1, "gmix", [128, D], F32); b_gmix = S.buf("gmix")
                        S.dma("sp", gmix[:], nmix_d.to_broadcast([128, D]), writes=[b_gmix])
                        xts = [sb(st1, f"xt{i}", [128, D], F32) for i in range(4)]
                        xring = Ring([(xts[i], S.buf("xt")) for i in range(4)])
                        xnbs = [sb(st1, f"xnb{i}", [128, D], BF16) for i in range(3)]
                        xnring = Ring([(xnbs[i], S.buf("xnb")) for i in range(3)])
                        def s1A(tt):
                            xt, bxt = xring.next()
                            S.dma("sp", xt[:], x[s, tt * 128:(tt + 1) * 128, :], writes=[bxt])
                            stt, bst = stat_ring.next()
                            xnb, bxnb = xnring.next()
                            S.op("act", lambda e: e.activation(out=xnb[:], in_=xt[:], func=AF.Square,
                                                               accum_out=stt[:, 0:1]),
                                 reads=[bxt], writes=[bxnb, bst])
                            S.op("act", lambda e: e.activation(out=stt[:, 1:2], in_=stt[:, 0:1], func=AF.Sqrt,
                                                               scale=1.0 / D, bias=EPS), reads=[bst], writes=[bst])
                            S.op("dve", lambda e: e.reciprocal(out=stt[:, 2:3], in_=stt[:, 1:2]), reads=[bst], writes=[bst])
                            S.op("dve", lambda e: e.scalar_tensor_tensor(out=xnb[:], in0=xt[:], scalar=stt[:, 2:3],
                                                                         in1=gmix[:], op0=ALU.mult, op1=ALU.mult),
                                 reads=[bxt, bst, b_gmix], writes=[bxnb])
                            return xnb, bxnb

                        def s1B(tt, xnb, bxnb):
                            pi = 6 + (tt % 2)
                            for kc in range(8):
                                S.op("pe", lambda e: e.transpose(out=psb(pi)[:, kc * 128:(kc + 1) * 128],
                                                                 in_=xnb[:, kc * 128:(kc + 1) * 128], identity=ident[:]),
                                     reads=[bxnb, b_ident], writes=[bPS[pi]])
                            S.op("act", lambda e: e.copy(out=xnT[:, :, tt * 128:(tt + 1) * 128],
                                                         in_=psb(pi)[:, 0:1024].rearrange("p (a b) -> p a b", a=8)),
                                 reads=[bPS[pi]], writes=[b_xnT[tt]])

                        prev = None
                        for tt in range(NT):
                            cur = s1A(tt)
                            if prev is not None:
                                s1B(tt - 1, *prev)
                            prev = cur
                        s1B(NT - 1, *prev)
                        S.barrier()
                    S.dma("sp", caus[:], caus_d, writes=[b_caus])
                    S.dma("sp", addc[:], addc_d, writes=[b_addc])
                    for g in range(4):
                        for r in range(4):
                            h = 4 * g + r
                            base = h * 128 * 512
                            S.dma("sp", B0[:, g, r * 128:(r + 1) * 128],
                                  bass.AP(mscr.tensor, base + OFF, [[511, 128], [1, 128]]), writes=[b_B0], own=b_mscr)
                            S.dma("sp", B1[:, g, r * 128:(r + 1) * 128],
                                  bass.AP(mscr.tensor, base + OFF + 128, [[511, 128], [1, 128]]), writes=[b_B1],
                                  own=b_mscr)
                            S.dma("sp", WTn[0:16, g, r * 128:(r + 1) * 128],
                                  bass.AP(mscr.tensor, base + 257, [[496, 16], [1, 128]]), writes=[b_WTn], own=b_mscr)
                    for kv in range(2):
                        pb = kv * 64
                        for mh in range(2):
                            idx = kv * 2 + mh
                            for l in range(32):
                                S.op("pe", lambda e: e.matmul(PS[kv][:, 2 * mh:2 * mh + 2],
                                                              lhsT=w1kv[pb:pb + 64, l, mh * 128:(mh + 1) * 128],
                                                              rhs=peT[pb:pb + 64, l:l + 1].to_broadcast([64, 2]),
                                                              start=(l == 0), stop=(l == 31)),
                                     reads=[b_w1kv, b_peT], writes=[bPS[kv]])
                    for kv in range(2):
                        S.op("act", lambda e: e.copy(out=cbias[:, 2 * kv:2 * kv + 2], in_=PS[kv][:, 0:4:2]),
                             reads=[bPS[kv]], writes=[b_cbias])
                    S.dma("pool", Wgn[:], w_in.rearrange("(kc p) c -> p kc c", p=128)[:, :, C_GN:C_GN + 48],
                          writes=[b_Wgn])
                    for tt in range(NT):
                        pi = tt % 2
                        for kc in range(8):
                            S.op("pe", lambda e: e.matmul(PS[pi][:, 0:48], lhsT=xnT[:, kc, tt * 128:(tt + 1) * 128],
                                                          rhs=Wgn[:, kc, :], start=(kc == 0), stop=(kc == 7)),
                                 reads=[b_Wgn, b_xnT[tt]], writes=[bPS[pi]])
                        S.op("act", lambda e: e.activation(out=GATES[:, tt, :], in_=PS[pi][:, 0:48], func=AF.Sigmoid),
                             reads=[bPS[pi]], writes=[b_G[tt]])

                    w_in_v = w_in.rearrange("(kc p) c -> p kc c", p=128)

                    def load_group_weights(g):
                        wg, bwg = wg_ring.next()
                        S.dma("pool", wg[:, :, 0:256], w_in_v[:, :, C_Q + 256 * g:C_Q + 256 * (g + 1)], writes=[bwg])
                        for i, c0 in enumerate((C_KS, C_KW, C_VS, C_VW)):
                            S.dma("pool", wg[:, :, 256 + 64 * i:256 + 64 * (i + 1)],
                                  w_in_v[:, :, c0 + 64 * g:c0 + 64 * (g + 1)], writes=[bwg])
                        wc, bwc = wc_ring.next()
                        S.dma("pool", wc[:, :, 0:64], w_in_v[:, :, C_KC + 64 * g:C_KC + 64 * (g + 1)], writes=[bwc])
                        S.dma("pool", wc[:, :, 64:128], w_in_v[:, :, C_VC + 64 * g:C_VC + 64 * (g + 1)], writes=[bwc])
                        return wg, bwg, wc, bwc

                    nxt = load_group_weights(0)
                    for g in range(DBG.get("groups", 4)):
                        wg, bwg, wc, bwc = nxt
                        if g + 1 < 4:
                            nxt = load_group_weights(g + 1)
                        def projA(tt):
                            pi = tt % 2
                            for kc in range(8):
                                S.op("pe", lambda e: e.matmul(PS[pi][:, :], lhsT=xnT[:, kc, tt * 128:(tt + 1) * 128],
                                                              rhs=wg[:, kc, :], start=(kc == 0), stop=(kc == 7)),
                                     reads=[bwg, b_xnT[tt]], writes=[bPS[pi]])
                            sq_, bsq = sq_ring.next()
                            S.op("act", lambda e: e.activation(out=sq_[:], in_=PS[pi][:, 0:384], func=AF.Square),
                                 reads=[bPS[pi]], writes=[bsq])
                            stt, bst = stat_ring.next()
                            S.op("dve", lambda e: e.tensor_reduce(out=stt[:, 0:6],
                                                                  in_=sq_[:, :].rearrange("p (a b) -> p a b", a=6),
                                                                  axis=AX.X, op=ALU.add), reads=[bsq], writes=[bst])
                            stt2, bst2 = stat_ring.next()
                            S.op("act", lambda e: e.activation(out=stt2[:, 0:4], in_=stt[:, 0:4], func=AF.Sqrt,
                                                               scale=1.0, bias=64.0 * EPS), reads=[bst], writes=[bst2])
                            S.op("act", lambda e: e.activation(out=stt2[:, 4:6], in_=stt[:, 4:6], func=AF.Sqrt,
                                                               scale=1.0 / 64, bias=EPS), reads=[bst], writes=[bst2])
                            S.op("dve", lambda e: e.reciprocal(out=stt[:, 0:6], in_=stt2[:, 0:6]),
                                 reads=[bst2], writes=[bst])
                            nr, bnr = nrm_ring.next()
                            S.op("dve", lambda e: e.tensor_tensor(
                                out=nr[:, :].rearrange("p (a b) -> p a b", a=6),
                                in0=PS[pi][:, 0:384].rearrange("p (a b) -> p a b", a=6),
                                in1=stt[:, 0:6].unsqueeze(2).to_broadcast([128, 6, 64]), op=ALU.mult),
                                 reads=[bPS[pi], bst], writes=[bnr])
                            S.op("dve", lambda e: e.tensor_copy(
                                out=Vsw[:, :, tt, 0:64],
                                in_=PS[pi][:, 384:512].rearrange("p (a b) -> p a b", a=2)),
                                 reads=[bPS[pi]], writes=[b_V[tt]])
                            return nr, bnr

                        def projB(tt, nr, bnr):
                            ti = 6 + (tt % 2)
                            for i in range(6):
                                S.op("pe", lambda e: e.transpose(out=psb(ti)[0:64, i * 128:(i + 1) * 128],
                                                                 in_=nr[:, i * 64:(i + 1) * 64], identity=ident[:]),
                                     reads=[bnr, b_ident], writes=[bPS[ti]])
                            S.op("act", lambda e: e.activation(
                                out=Qaug[0:64, tt], in_=psb(ti)[0:64, 0:512].rearrange("p (a b) -> p a b", a=4),
                                func=AF.Copy, scale=gains[:, 0:1]), reads=[bPS[ti], b_gains], writes=[b_Q[tt]])
                            S.op("act", lambda e: e.activation(
                                out=KsT[0:64, tt * 128:(tt + 1) * 128], in_=psb(ti)[0:64, 512:640],
                                func=AF.Copy, scale=gains[:, 2:3]), reads=[bPS[ti], b_gains], writes=[b_KsT[tt]])
                            S.op("act", lambda e: e.activation(
                                out=KwT[0:64, tt * 128:(tt + 1) * 128], in_=psb(ti)[0:64, 640:768],
                                func=AF.Copy, scale=gains[:, 3:4]), reads=[bPS[ti], b_gains], writes=[b_KwT[tt]])

                        prevA = None
                        for tt in range(NT):
                            curA = projA(tt)
                            if prevA is not None:
                                projB(tt - 1, *prevA)
                            prevA = curA
                        projB(NT - 1, *prevA)
                        if DBG.get("skip_phi"):
                            continue
                        for t4 in range(4):
                            pi = t4 % 2
                            proj_fm(pi, wc, slice(0, 128), 128, t4, bwc)
                            S.op("act", lambda e: e.copy(out=KVcT[:, :, t4 * 32:(t4 + 1) * 32],
                                                         in_=PS[pi][:, :].rearrange("p (c s) -> p s c", s=16)),
                                 reads=[bPS[pi]], writes=[b_KVc])
                        if DBG.get("phi_stop", 99) <= 1:
                            continue
                        for kv in range(2):
                            pb = kv * 64
                            for mh in range(2):
                                idx = kv * 2 + mh
                                for l in range(32):
                                    S.op("pe", lambda e: e.matmul(
                                        PS[kv][:, mh * 128:(mh + 1) * 128],
                                        lhsT=w1kv[pb:pb + 64, l, mh * 128:(mh + 1) * 128],
                                        rhs=(KVcT[pb:pb + 64, l, 0:128] if l < 16 else KVcT[pb:pb + 64, l - 16, 1:129]),
                                        start=(l == 0), stop=(l == 31)),
                                         reads=[b_w1kv, b_KVc], writes=[bPS[kv]])
                        if DBG.get("phi_stop", 99) <= 2:
                            continue
                        for idx in range(4):
                            S.op("act", lambda e: e.activation(out=HT[:, idx, :],
                                                               in_=PS[idx // 2][:, (idx % 2) * 128:(idx % 2 + 1) * 128],
                                                               func=AF.Gelu_apprx_tanh, bias=cbias[:, idx:idx + 1]),
                                 reads=[bPS[idx // 2], b_cbias], writes=[b_HT])
                        if DBG.get("phi_stop", 99) <= 3:
                            continue
                        for kv in range(2):
                            for mh in range(2):
                                S.op("pe", lambda e: e.matmul(PS[2][:, kv * 64:(kv + 1) * 64],
                                                              lhsT=HT[:, kv * 2 + mh, :], rhs=w2kv[:, kv, mh, :],
                                                              start=(mh == 0), stop=(mh == 1)),
                                     reads=[b_HT, b_w2kv], writes=[bPS[2]])
                        if DBG.get("phi_stop", 99) <= 4:
                            continue
                        stt, bst = stat_ring.next()
                        sq_, bsq = sq_ring.next()
                        S.op("act", lambda e: e.activation(out=sq_[:, 0:64], in_=PS[2][:, 0:64], func=AF.Square,
                                                           accum_out=stt[:, 0:1]), reads=[bPS[2]], writes=[bsq, bst])
                        S.op("act", lambda e: e.activation(out=stt[:, 1:2], in_=stt[:, 0:1], func=AF.Sqrt,
                                                           scale=1.0 / 64, bias=EPS), reads=[bst], writes=[bst])
                        S.op("dve", lambda e: e.reciprocal(out=stt[:, 2:3], in_=stt[:, 1:2]),
                             reads=[bst], writes=[bst])
                        S.op("dve", lambda e: e.tensor_scalar(out=kcn[:, :], in0=PS[2][:, 0:64],
                                                              scalar1=stt[:, 2:3], scalar2=None, op0=ALU.mult),
                             reads=[bPS[2], bst], writes=[b_kcn])
                        S.op("act", lambda e: e.copy(out=VCaug[:, 0:64], in_=PS[2][:, 64:128]),
                             reads=[bPS[2]], writes=[b_VC])
                        if DBG.get("phi_stop", 99) <= 5:
                            continue
                        S.op("pe", lambda e: e.transpose(out=psb(6)[0:64, 0:128], in_=kcn[:, :],
                                                         identity=ident[:]),
                             reads=[b_kcn, b_ident], writes=[bPS[6]])
                        S.op("act", lambda e: e.activation(out=KcT[:, 0:128], in_=psb(6)[0:64, 0:128], func=AF.Copy,
                                                           scale=gains[:, 1:2]), reads=[bPS[6], b_gains], writes=[b_KcT])

                        OcV = PS[2][:, 0:388].rearrange("p (r c) -> p r c", r=4)
                        OsV = PS[3][:, 0:260].rearrange("p (r c) -> p r c", r=4)
                        OwV = PS[4][:, 0:260].rearrange("p (r c) -> p r c", r=4)
                        NQ = DBG.get("qts", NT)
                        items = []
                        for qq in range(NQ + 1):
                            if qq < NQ:
                                items.append(("c", qq, 0, True, True))
                                k0 = max(0, qq - 4)
                                items += [("w", qq, kt, kt == k0, kt == qq) for kt in range(k0, qq + 1)]
                            if qq >= 1:
                                items += [("s", qq - 1, kt, kt == 0, kt == qq - 1) for kt in range(0, qq)]
                        qst = {}
                        deferred_tail = []

                        def qstate(qt):
                            if qt not in qst:
                                Y, bY = y_ring.next()
                                yb, byb = yb_ring.next()
                                nw, bnw = nw_ring.next()
                                qst[qt] = dict(
                                    Y=Y, bY=bY, yb=yb, byb=byb, nw=nw, bnw=bnw,
                                    Q64=Qaug[0:64, qt].rearrange("p a b -> p (a b)"),
                                    Q96=Qaug[0:96, qt].rearrange("p a b -> p (a b)"),
                                    gat=GATES[:, qt, :].rearrange("p (h j) -> p h j", j=3))
                            return qst[qt]

                        def emit_qk(kind, qt, kt, si):
                            q = qstate(qt)
                            if kind == "c":
                                S.op("pe", lambda e: e.matmul(PS[si][:, :], lhsT=KcT[:, 0:128], rhs=q["Q64"],
                                                              start=True, stop=False),
                                     reads=[b_KcT, b_Q[qt]], writes=[bPS[si]])
                                S.op("pe", lambda e: e.matmul(PS[si][:, :], lhsT=SelC[:, qt, :],
                                                              rhs=WTn[:, g, :], start=False, stop=True),
                                     reads=[b_SelC, b_WTn, b_WTf], writes=[bPS[si]])
                                return
                            dl = qt - kt
                            if kind == "w":
                                bt = B0[:, g, :] if dl == 0 else B1[:, g, :] if dl == 1 else B4[:, :] if dl == 4 else None
                                bb = b_B0 if dl == 0 else b_B1 if dl == 1 else b_B4
                                S.op("pe", lambda e: e.matmul(PS[si][:, :], lhsT=KwT[:, kt * 128:(kt + 1) * 128],
                                                              rhs=q["Q64"], start=True, stop=(bt is None)),
                                     reads=[b_KwT[kt], b_Q[qt]], writes=[bPS[si]])
                            else:
                                bt = B0[:, g, :] if dl == 0 else B1[:, g, :] if dl == 1 else None
                                bb = b_B0 if dl == 0 else b_B1
                                S.op("pe", lambda e: e.matmul(PS[si][:, :], lhsT=KsT[:, kt * 128:(kt + 1) * 128],
                                                              rhs=q["Q96"], start=True, stop=(bt is None)),
                                     reads=[b_KsT[kt], b_Q[qt]], writes=[bPS[si]])
                            if bt is not None:
                                S.op("pe", lambda e: e.matmul(PS[si][:, :], lhsT=ident[:], rhs=bt,
                                                              start=False, stop=True),
                                     reads=[b_ident, bb], writes=[bPS[si]])

                        def emit_exp(si):
                            P, bP = p_ring.next()
                            S.op("act", lambda e: e.activation(out=P[:, :], in_=PS[si][:, :], func=AF.Exp),
                                 reads=[bPS[si]], writes=[bP])
                            return P, bP

                        def finish_branch(j, qt, OV, oi):
                            q = qstate(qt)
                            Y, bY, yb, byb, gat = q["Y"], q["bY"], q["yb"], q["byb"], q["gat"]
                            stt, bst = stat_ring.next()
                            if j == 0:
                                S.op("dve", lambda e: e.tensor_scalar(out=stt[:, 0:4], in0=OV[:, :, 64], scalar1=1e-30,
                                                                      scalar2=None, op0=ALU.max),
                                     reads=[bPS[oi]], writes=[bst])
                                S.op("dve", lambda e: e.reciprocal(out=stt[:, 4:8], in_=stt[:, 0:4]),
                                     reads=[bst], writes=[bst])
                            else:
                                S.op("dve", lambda e: e.reciprocal(out=stt[:, 4:8], in_=OV[:, :, 64]),
                                     reads=[bPS[oi]], writes=[bst])
                            stg, bsg = stat_ring.next()
                            S.op("dve", lambda e: e.tensor_tensor(out=stg[:, 0:4], in0=stt[:, 4:8],
                                                                  in1=gat[:, 4 * g:4 * g + 4, j], op=ALU.mult),
                                 reads=[bst, b_G[qt]], writes=[bsg])
                            for r in range(4):
                                if j == 0:
                                    S.op("dve", lambda e: e.tensor_scalar(out=Y[:, r, :], in0=OV[:, r, 0:64],
                                                                          scalar1=stg[:, r:r + 1], scalar2=None,
                                                                          op0=ALU.mult),
                                         reads=[bPS[oi], bsg], writes=[bY[r]])
                                elif j == 2:
                                    S.op("dve", lambda e: e.scalar_tensor_tensor(
                                        out=Y[:, r, :], in0=OV[:, r, 0:64], scalar=stg[:, r:r + 1], in1=Y[:, r, :],
                                        op0=ALU.mult, op1=ALU.add),
                                         reads=[bPS[oi], bsg, bY[r]], writes=[bY[r]])
                                else:
                                    S.op("dve", lambda e: e.scalar_tensor_tensor(
                                        out=yb[:, r * 64:(r + 1) * 64], in0=OV[:, r, 0:64], scalar=stg[:, r:r + 1],
                                        in1=Y[:, r, :], op0=ALU.mult, op1=ALU.add),
                                         reads=[bPS[oi], bsg, bY[r]], writes=[byb])
                            return stt, bst

                        def emit_pv(kind, qt, kt, first, last, P, bP):
                            q = qstate(qt)
                            if kind == "c":
                                for r in range(4):
                                    S.op("pe", lambda e: e.matmul(PS[2][:, r * 97:(r + 1) * 97],
                                                                  lhsT=P[:, r * 128:(r + 1) * 128],
                                                                  rhs=VCaug[:, :], start=True, stop=True),
                                         reads=[bP, b_VC], writes=[bPS[2]])
                                stt, bst = finish_branch(0, qt, OcV, 2)
                                imp, bimp = sm_ring.next()
                                for r in range(4):
                                    if r == 0:
                                        S.op("dve", lambda e: e.tensor_scalar(out=imp[:, 0:32], in0=OcV[:, 0, 65:97],
                                                                              scalar1=stt[:, 4:5], scalar2=None,
                                                                              op0=ALU.mult),
                                             reads=[bPS[2], bst], writes=[bimp])
                                    else:
                                        S.op("dve", lambda e: e.scalar_tensor_tensor(
                                            out=imp[:, 0:32], in0=OcV[:, r, 65:97], scalar=stt[:, 4 + r:5 + r],
                                            in1=imp[:, 0:32], op0=ALU.mult, op1=ALU.add),
                                             reads=[bPS[2], bst, bimp], writes=[bimp])
                                S.op("dve", lambda e: e.tensor_tensor(out=imp[:, 0:32], in0=imp[:, 0:32],
                                                                      in1=caus[:, qt, :], op=ALU.mult),
                                     reads=[bimp, b_caus], writes=[bimp])
                                S.op("dve", lambda e: e.tensor_tensor(out=imp[:, 0:32], in0=imp[:, 0:32],
                                                                      in1=addc[:, qt, :], op=ALU.add),
                                     reads=[bimp, b_addc], writes=[bimp])
                                m8, bm8 = stat_ring.next()
                                S.op("dve", lambda e: e.max(out=m8[:, 0:8], in_=imp[:, 0:32]), reads=[bimp], writes=[bm8])
                                S.op("dve", lambda e: e.match_replace(out=imp[:, 32:64], in_to_replace=m8[:, 0:8],
                                                                      in_values=imp[:, 0:32], imm_value=-2.0),
                                     reads=[bimp, bm8], writes=[bimp])
                                m8b, bm8b = stat_ring.next()
                                S.op("dve", lambda e: e.max(out=m8b[:, 0:8], in_=imp[:, 32:64]), reads=[bimp],
                                     writes=[bm8b])
                                S.op("dve", lambda e: e.tensor_scalar(out=m8[:, 0:1], in0=m8b[:, 7:8], scalar1=0.0,
                                                                      scalar2=None, op0=ALU.max),
                                     reads=[bm8b], writes=[bm8])
                                S.op("dve", lambda e: e.tensor_scalar(out=q["nw"][:, 64:96], in0=imp[:, 0:32],
                                                                      scalar1=m8[:, 0:1], scalar2=1.0,
                                                                      op0=ALU.is_ge, op1=ALU.subtract),
                                     reads=[bimp, bm8], writes=[q["bnw"]])
                                q["c_done"] = True
                                return
                            oi = 3 if kind == "s" else 4
                            vi = 0 if kind == "s" else 1
                            for r in range(4):
                                S.op("pe", lambda e: e.matmul(PS[oi][:, r * 65:(r + 1) * 65],
                                                              lhsT=P[:, r * 128:(r + 1) * 128], rhs=Vsw[:, vi, kt, :],
                                                              start=(first and r == 0), stop=last, skip_group_check=True),
                                     reads=[bP, b_V[kt]], writes=[bPS[oi]])
                            if last:
                                finish_branch(1 if kind == "s" else 2, qt, OsV if kind == "s" else OwV, oi)
                                if kind == "s":
                                    def make_tail(yb=q["yb"], byb=q["byb"], g=g, qt=qt):
                                        def tail():
                                            for i in range(2):
                                                S.op("pe", lambda e: e.transpose(out=psb(6)[:, i * 128:(i + 1) * 128],
                                                                                 in_=yb[:, i * 128:(i + 1) * 128],
                                                                                 identity=ident[:]),
                                                     reads=[byb, b_ident], writes=[bPS[6]])
                                            S.op("dve", lambda e: e.tensor_copy(
                                                out=ybT[:, 2 * g:2 * g + 2, qt * 128:(qt + 1) * 128],
                                                in_=psb(6)[:, 0:256].rearrange("p (a b) -> p a b", a=2)),
                                                 reads=[bPS[6]], writes=[b_ybT[qt]])
                                        return tail
                                    deferred_tail.append(make_tail())
                                    del qst[qt]

                        def emit_mask(qt):
                            q = qstate(qt)
                            S.op("pe", lambda e: e.transpose(out=psb(7)[0:96, 0:128], in_=q["nw"][:, 0:96],
                                                             identity=ident[:]),
                                 reads=[q["bnw"], b_ident], writes=[bPS[7]])
                            S.op("dve", lambda e: e.tensor_scalar(
                                out=Qaug[64:96, qt],
                                in0=psb(7)[64:96, 0:128].unsqueeze(1).to_broadcast([32, 4, 128]), scalar1=-NEG,
                                scalar2=None, op0=ALU.mult),
                                 reads=[bPS[7]], writes=[b_Q[qt]])

                        def need_mask(mq):
                            if mq < NQ and mq not in mask_done:
                                while any(p[0] == "c" and p[1] == mq for p in pend):
                                    emit_pv(*pend.pop(0))
                                emit_mask(mq)
                                mask_done.add(mq)

                        pend = []
                        mask_done = set()
                        sbanks = [5, 1, 0]
                        DEPTH = 2
                        for idx, (kind, qt, kt, first, last) in enumerate(items):
                            if kind == "s" and first:
                                need_mask(qt)
                                while deferred_tail:
                                    deferred_tail.pop(0)()
                            si = sbanks[idx % 3]
                            emit_qk(kind, qt, kt, si)
                            P, bP = emit_exp(si)
                            pend.append((kind, qt, kt, first, last, P, bP))
                            if len(pend) > DEPTH:
                                emit_pv(*pend.pop(0))
                            if kind == "s" and last:
                                need_mask(qt + 1)
                        while pend:
                            emit_pv(*pend.pop(0))
                        while deferred_tail:
                            deferred_tail.pop(0)()
                    S.barrier()
                if debug:
                    S.dma("sp", dbg["d_ybT"], ybT[:], reads=b_ybT, own=b_ybT[0])
                if stop_after == "nsa":
                    S.barrier()
                    sx.close()
                    sy.close()
                    continue

                mgT = sb(sq, f"mgT{s}", [128, 8, T], BF16)
                b_mgT = [S.buf("mgT") for _ in range(4)]
                with ExitStack() as st2:
                    Wp = [sb(st2, f"Wpb{i}", [128, 8, 256], BF16) for i in range(2)]
                    Wgb = [sb(st2, f"Wgb{i}", [128, 8, 256], BF16) for i in range(2)]
                    wp_ring = Ring([(Wp[i], S.buf("Wpb"), Wgb[i], S.buf("Wgb")) for i in range(2)])
                    sg = [sb(st2, f"sgb{i}", [128, 512], F32) for i in range(2)]
                    sg_ring = Ring([(sg[i], S.buf("sgb")) for i in range(2)])
                    tmpb = [sb(st2, f"tmpb{i}", [128, 512], F32) for i in range(2)]
                    tmp_ring = Ring([(tmpb[i], S.buf("tmpb")) for i in range(2)])
                    pj = Ring([0, 1, 2, 3, 4, 5])
                    w_in_v = w_in.rearrange("(kc p) c -> p kc c", p=128)
                    pb_v = proj_b.rearrange("(kc p) o -> p kc o", p=128)

                    def load_pb(c4):
                        wp, bwp, wgb, bwgb = wp_ring.next()
                        S.dma("pool", wp[:], pb_v[:, :, c4 * 256:(c4 + 1) * 256], writes=[bwp])
                        S.dma("pool", wgb[:], w_in_v[:, :, C_GB + c4 * 256:C_GB + (c4 + 1) * 256], writes=[bwgb])
                        return wp, bwp, wgb, bwgb

                    nxt = load_pb(0)
                    for c4 in range(4):
                        wp, bwp, wgb, bwgb = nxt
                        if c4 + 1 < 4:
                            nxt = load_pb(c4 + 1)
                        for f2 in range(2):
                            fc = c4 * 2 + f2
                            for t4 in range(4):
                                pg = pj.next()
                                proj_fm(pg, wgb, slice(f2 * 128, (f2 + 1) * 128), 128, t4, bwgb)
                                sg_, bsg = sg_ring.next()
                                S.op("act", lambda e: e.activation(out=sg_[:], in_=PS[pg][:, :], func=AF.Sigmoid),
                                     reads=[bPS[pg]], writes=[bsg])
                                pp = pj.next()
                                for kc in range(8):
                                    S.op("pe", lambda e: e.matmul(PS[pp][:, :], lhsT=wp[:, kc, f2 * 128:(f2 + 1) * 128],
                                                                  rhs=ybT[:, kc, t4 * 512:(t4 + 1) * 512],
                                                                  start=(kc == 0), stop=(kc == 7)),
                                         reads=[bwp] + b_ybT[t4 * 4:(t4 + 1) * 4], writes=[bPS[pp]])
                                S.op("dve", lambda e: e.tensor_tensor(out=mgT[:, fc, t4 * 512:(t4 + 1) * 512],
                                                                      in0=PS[pp][:, :], in1=sg_[:], op=ALU.mult),
                                     reads=[bPS[pp], bsg], writes=[b_mgT[t4]])
                    S.barrier()
                if debug:
                    S.dma("sp", dbg["d_paT"], mgT[:], reads=b_mgT, own=b_mgT[0])
                    S.barrier()
                sy.close()
                if stop_after == "merge":
                    sx.close()
                    continue

                with ExitStack() as st:
                    yaT = sb(st, "yaT", [112, 12, T], BF16)
                    b_yaT = [[S.buf("yaT") for _ in range(4)] for _ in range(12)]
                    with ExitStack() as st2:
                        QW = 512
                        Wu = [sb(st2, f"Wu{i}", [128, 8, 336], BF16) for i in range(2)]
                        Wv = [sb(st2, f"Wv{i}", [128, 8, 336], BF16) for i in range(2)]
                        wu_ring = Ring([(Wu[i], S.buf("Wu"), Wv[i], S.buf("Wv")) for i in range(2)])
                        Wa = sb(st2, "Wa", [112, 3, 336], BF16); bWa = S.buf("Wa")
                        Wx = sb(st2, "Wx", [112, 3, 336], BF16); bWx = S.buf("Wx")
                        upad = [sb(st2, f"upad{j}", [112, 3 + QW], F32) for j in range(3)]
                        b_upad = [S.buf("upad") for _ in range(3)]
                        csets = []
                        for i in range(2):
                            csets.append(dict(
                                xc=[sb(st2, f"xc{i}{j}", [112, QW], F32) for j in range(3)],
                                b_xc=[S.buf("xc") for _ in range(3)],
                                xcb=[sb(st2, f"xcb{i}{j}", [112, QW], BF16) for j in range(3)],
                                b_xcb=[S.buf("xcb") for _ in range(3)]))
                        rsets = []
                        for i in range(2):
                            rsets.append(dict(
                                R=[sb(st2, f"Rt{i}{j}", [112, QW], F32) for j in range(3)],
                                I=[sb(st2, f"It{i}{j}", [112, QW], F32) for j in range(3)],
                                M=[sb(st2, f"Mt{i}{j}", [112, QW], F32) for j in range(3)],
                                bR=[S.buf("Rt") for _ in range(3)],
                                bI=[S.buf("It") for _ in range(3)],
                                bM=[S.buf("Mt") for _ in range(3)]))
                        hcar = sb(st2, "hcar", [112, 12], F32); b_hcar = [S.buf("hcar") for _ in range(12)]
                        w_in_v = w_in.rearrange("(kc p) c -> p kc c", p=128)
                        blkw = {}

                        def load_uv(n):
                            wu, bwu, wv, bwv = wu_ring.next()
                            S.dma("pool", wu[:], w_in_v[:, :, C_URNN + 336 * n:C_URNN + 336 * (n + 1)], writes=[bwu])
                            S.dma("pool", wv[:], w_in_v[:, :, C_UGATE + 336 * n:C_UGATE + 336 * (n + 1)], writes=[bwv])
                            blkw[n] = (wu, bwu, wv, bwv)

                        def load_ax(n):
                            S.dma("pool", Wa[:], gaw[n].rearrange("(i p) o -> p i o", p=112), writes=[bWa])
                            S.dma("pool", Wx[:], gxw[n].rearrange("(i p) o -> p i o", p=112), writes=[bWx])

                        pj = Ring([0, 1, 2, 3, 4, 5])

                        def conv_stage(k):
                            n, th = divmod(k, 4)
                            cs = csets[k % 2]
                            wu, bwu = blkw[n][0], blkw[n][1]
                            for j in range(3):
                                up, bup = upad[j], b_upad[j]
                                if th == 0:
                                    S.op("pool", lambda e: e.memset(up[:, 0:3], 0.0), writes=[bup])
                                else:
                                    S.op("act", lambda e: e.copy(out=up[:, 0:3], in_=up[:, QW:QW + 3]),
                                         reads=[bup], writes=[bup])
                                pi = pj.next()
                                proj_fm(pi, wu, slice(112 * j, 112 * (j + 1)), 112, th, bwu)
                                S.op("act", lambda e: e.copy(out=up[:, 3:3 + QW], in_=PS[pi][0:112, :]),
                                     reads=[bPS[pi]], writes=[bup])
                            for j in range(3):
                                ci = 3 * n + j
                                up, bup = upad[j], b_upad[j]
                                xc_, bxc = cs["xc"][j], cs["b_xc"][j]
                                S.op("dve", lambda e: e.tensor_scalar(out=xc_[:], in0=up[:, 3:3 + QW],
                                                                      scalar1=chtab[:, ci, 3:4], scalar2=chtab[:, ci, 4:5],
                                                                      op0=ALU.mult, op1=ALU.add),
                                     reads=[bup, b_chtab], writes=[bxc])
                                for kk in range(3):
                                    S.op("dve", lambda e: e.scalar_tensor_tensor(
                                        out=xc_[:], in0=up[:, kk:kk + QW], scalar=chtab[:, ci, kk:kk + 1],
                                        in1=xc_[:], op0=ALU.mult, op1=ALU.add),
                                         reads=[bup, b_chtab, bxc], writes=[bxc])

                        def conv_cast(k):
                            cs = csets[k % 2]
                            for j in range(3):
                                S.op("act", lambda e: e.copy(out=cs["xcb"][j][:], in_=cs["xc"][j][:]),
                                     reads=[cs["b_xc"][j]], writes=[cs["b_xcb"][j]])

                        def mainA(k):
                            n, th = divmod(k, 4)
                            cs = csets[k % 2]
                            rs = rsets[k % 2]
                            Rt, It, Mt, bR, bI, bM = rs["R"], rs["I"], rs["M"], rs["bR"], rs["bI"], rs["bM"]
                            for j in range(3):
                                ci = 3 * n + j
                                for (wt, bwt, dst, bdst, col) in ((Wa, bWa, Rt[j], bR[j], 0), (Wx, bWx, It[j], bI[j], 1)):
                                    pr = pj.next()
                                    for i in range(3):
                                        S.op("pe", lambda e: e.matmul(PS[pr][0:112, :],
                                                                      lhsT=wt[:, i, 112 * j:112 * (j + 1)],
                                                                      rhs=cs["xcb"][i][:, :], start=(i == 0), stop=(i == 2)),
                                             reads=[bwt, cs["b_xcb"][i]], writes=[bPS[pr]])
                                    S.op("act", lambda e: e.activation(out=dst[:], in_=PS[pr][0:112, :], func=AF.Tanh,
                                                                       bias=hbias[:, ci, col:col + 1], scale=0.5),
                                         reads=[bPS[pr], b_hbias], writes=[bdst])
                            for j in range(3):
                                ci = 3 * n + j
                                S.op("act", lambda e: e.activation(out=Rt[j][:], in_=Rt[j][:], func=AF.Exp,
                                                                   scale=cvec[:, ci, 1:2], bias=cvec[:, ci, 1:2]),
                                     reads=[bR[j], b_cvec], writes=[bR[j]])
                                S.op("pool", lambda e: e.tensor_tensor(out=Mt[j][:], in0=Rt[j][:], in1=Rt[j][:], op=ALU.mult),
                                     reads=[bR[j]], writes=[bM[j]])
                            for j in range(3):
                                S.op("act", lambda e: e.activation(out=Mt[j][:], in_=Mt[j][:], func=AF.Sqrt,
                                                                   scale=-0.25, bias=0.25),
                                     reads=[bM[j]], writes=[bM[j]])
                            for j in range(3):
                                S.op("dve", lambda e: e.scalar_tensor_tensor(out=It[j][:], in0=It[j][:], scalar=1.0,
                                                                             in1=cs["xc"][j][:], op0=ALU.add, op1=ALU.mult),
                                     reads=[bI[j], cs["b_xc"][j]], writes=[bI[j]])
                                if th == 0:
                                    S.op("dve", lambda e: e.memset(Mt[j][:, 0:1], 0.5), writes=[bM[j]])
                            for j in range(3):
                                S.op("pool", lambda e: e.tensor_tensor(out=Mt[j][:], in0=Mt[j][:], in1=It[j][:], op=ALU.mult),
                                     reads=[bI[j], bM[j]], writes=[bM[j]])
                            for j in range(3):
                                ci = 3 * n + j
                                S.op("dve", lambda e: e.tensor_tensor_scan(
                                    out=It[j][:], data0=Rt[j][:], data1=Mt[j][:],
                                    initial=(0.0 if th == 0 else hcar[:, ci:ci + 1]), op0=ALU.mult, op1=ALU.add),
                                     reads=[bR[j], bM[j], b_hcar[ci]], writes=[bI[j]])
                                if th < 3:
                                    S.op("dve", lambda e: e.tensor_copy(out=hcar[:, ci:ci + 1], in_=It[j][:, QW - 1:QW]),
                                         reads=[bI[j]], writes=[b_hcar[ci]])

                        def mainB(k):
                            n, th = divmod(k, 4)
                            rs = rsets[k % 2]
                            Rt, It, bR, bI = rs["R"], rs["I"], rs["bR"], rs["bI"]
                            wv, bwv = blkw[n][2], blkw[n][3]
                            for j in range(3):
                                pg = pj.next()
                                proj_fm(pg, wv, slice(112 * j, 112 * (j + 1)), 112, th, bwv)
                                S.op("act", lambda e: e.activation(out=Rt[j][:], in_=PS[pg][0:112, :],
                                                                   func=AF.Gelu_apprx_tanh),
                                     reads=[bPS[pg]], writes=[bR[j]])
                            for j in range(3):
                                ci = 3 * n + j
                                S.op("dve", lambda e: e.tensor_tensor(out=yaT[:, ci, th * QW:(th + 1) * QW], in0=It[j][:],
                                                                      in1=Rt[j][:], op=ALU.mult),
                                     reads=[bR[j], bI[j]], writes=[b_yaT[ci][th]])

                        load_uv(0)
                        load_ax(0)
                        conv_stage(0)
                        conv_cast(0)
                        for k in range(16):
                            n, th = divmod(k, 4)
                            if k + 1 < 16:
                                conv_stage(k + 1)
                            mainA(k)
                            if k + 1 < 16:
                                conv_cast(k + 1)
                            if th == 3 and n + 1 < 4:
                                load_ax(n + 1)
                            if k >= 1:
                                mainB(k - 1)
                            if th == 0 and n + 1 < 4:
                                load_uv(n + 1)
                        mainB(15)
                        S.barrier()
                    if debug:
                        S.dma("sp", dbg["d_yaT"], yaT[:], reads=[b for l in b_yaT for b in l], own=b_yaT[0][0])
                    with ExitStack() as st2:
                        Wp = [sb(st2, f"Wp{i}", [112, 12, 256], BF16) for i in range(2)]
                        Wga = [sb(st2, f"Wga{i}", [128, 8, 256], BF16) for i in range(2)]
                        wp_ring = Ring([(Wp[i], S.buf("Wp"), Wga[i], S.buf("Wga")) for i in range(2)])
                        sg = [sb(st2, f"sg{i}", [128, 512], F32) for i in range(2)]
                        sg_ring = Ring([(sg[i], S.buf("sg")) for i in range(2)])
                        tmpa = [sb(st2, f"tmpa{i}", [128, 512], F32) for i in range(2)]
                        tmp_ring = Ring([(tmpa[i], S.buf("tmpa")) for i in range(2)])
                        pj = Ring([0, 1, 2, 3, 4, 5])

                        def load_pa(c4):
                            wp, bwp, wga, bwga = wp_ring.next()
                            S.dma("pool", wp[:], proj_a.rearrange("(i p) o -> p i o", p=112)[:, :, c4 * 256:(c4 + 1) * 256],
                                  writes=[bwp])
                            S.dma("pool", wga[:], w_in_v[:, :, C_GA + c4 * 256:C_GA + (c4 + 1) * 256], writes=[bwga])
                            return wp, bwp, wga, bwga

                        nxt = load_pa(0)
                        for c4 in range(4):
                            wp, bwp, wga, bwga = nxt
                            if c4 + 1 < 4:
                                nxt = load_pa(c4 + 1)
                            if c4 == 2:
                                stT = ExitStack()
                                gmlp = sb(stT, "gmlp", [128, D], F32, top=True); b_gmlp = S.buf("gmlp")
                                S.dma("sp", gmlp[:], nmlp_d.to_broadcast([128, D]), writes=[b_gmlp])
                                wo = sb(stT, "wo", [128, 8, D], BF16, top=True); b_wo = S.buf("wo")
                                S.dma("pool", wo[:, 0:4, :], w_out.rearrange("(kc p) o -> p kc o", p=128)[:, 0:4, :], writes=[b_wo])
                                S.dma("pool", wo[:, 4:8, :], w_out.rearrange("(kc p) o -> p kc o", p=128)[:, 4:8, :], writes=[b_wo])
                            for f2 in range(2):
                                fc = c4 * 2 + f2
                                for t4 in range(4):
                                    pg = pj.next()
                                    proj_fm(pg, wga, slice(f2 * 128, (f2 + 1) * 128), 128, t4, bwga)
                                    sg_, bsg = sg_ring.next()
                                    S.op("act", lambda e: e.activation(out=sg_[:], in_=PS[pg][:, :], func=AF.Sigmoid),
                                         reads=[bPS[pg]], writes=[bsg])
                                    pp = pj.next()
                                    for i in range(12):
                                        S.op("pe", lambda e: e.matmul(PS[pp][:, :], lhsT=wp[:, i, f2 * 128:(f2 + 1) * 128],
                                                                      rhs=yaT[:, i, t4 * 512:(t4 + 1) * 512],
                                                                      start=(i == 0), stop=(i == 11)),
                                             reads=[bwp, b_yaT[i][t4]], writes=[bPS[pp]])
                                    tm, btm = tmp_ring.next()
                                    S.op("dve", lambda e: e.tensor_tensor(out=tm[:], in0=PS[pp][:, :], in1=sg_[:], op=ALU.mult),
                                         reads=[bPS[pp], bsg], writes=[btm])
                                    S.op("pool", lambda e: e.tensor_tensor(out=mgT[:, fc, t4 * 512:(t4 + 1) * 512],
                                                                           in0=mgT[:, fc, t4 * 512:(t4 + 1) * 512], in1=tm[:],
                                                                           op=ALU.add),
                                         reads=[btm, b_mgT[t4]], writes=[b_mgT[t4]])
                        S.barrier()
                if debug:
                    S.dma("sp", dbg["d_mgT"], mgT[:], reads=b_mgT, own=b_mgT[0])
                    S.barrier()
                sx.close()
                if stop_after == "rglru":
                    stT.close()
                    continue

                with stT as st2:
                    HT_ = 1024
                    hnT = sb(st2, "hnT", [128, 8, HT_], BF16); b_hnT = [S.buf("hnT") for _ in range(8)]
                    AT = sb(st2, "AT", [128, 32, HT_], BF16); b_AT = [[S.buf("AT") for _ in range(2)] for _ in range(32)]
                    wmi = [sb(st2, f"wmi{i}", [128, 8, 256], BF16) for i in range(2)]
                    wmi_ring = Ring([(wmi[i], S.buf("wmi")) for i in range(2)])
                    wmo = [sb(st2, f"wmo{i}", [128, 32, 256], BF16) for i in range(2)]
                    wmo_ring = Ring([(wmo[i], S.buf("wmo")) for i in range(2)])
                    xts = [sb(st2, f"xt2{i}", [128, D], F32) for i in range(2)]
                    xring = Ring([(xts[i], S.buf("xt2")) for i in range(2)])
                    hts = [sb(st2, f"ht{i}", [128, D], F32) for i in range(2)]
                    hring = Ring([(hts[i], S.buf("ht")) for i in range(2)])
                    hnb = [sb(st2, f"hnb{i}", [128, D], BF16) for i in range(2)]
                    hnring = Ring([(hnb[i], S.buf("hnb")) for i in range(2)])
                    rl = [sb(st2, f"rl{i}", [128, 512], F32) for i in range(2)]
                    rl_ring = Ring([(rl[i], S.buf("rl")) for i in range(2)])
                    xq = [sb(st2, f"xq{i}", [128, 256], F32) for i in range(2)]
                    xq_ring = Ring([(xq[i], S.buf("xq")) for i in range(2)])
                    oq = [sb(st2, f"oq{i}", [128, 256], F32) for i in range(2)]
                    oq_ring = Ring([(oq[i], S.buf("oq")) for i in range(2)])
                    pj = Ring([0, 1, 2, 3, 4, 5])
                    wmi_v = w_mi.rearrange("(kc p) f -> p kc f", p=128)
                    wmo_v = w_mo.rearrange("(fc p) o -> p fc o", p=128)
                    for hh in range(2):
                        tok0 = hh * HT_
                        def hA(tl):
                            tt = hh * 8 + tl
                            xt, bxt = xring.next()
                            S.dma("sp", xt[:], x[s, tt * 128:(tt + 1) * 128, :], writes=[bxt])
                            ht, bht = hring.next()
                            for ch in range(2):
                                pi = pj.next()
                                for kc in range(8):
                                    S.op("pe", lambda e: e.matmul(PS[pi][:, :], lhsT=mgT[:, kc, tt * 128:(tt + 1) * 128],
                                                                  rhs=wo[:, kc, ch * 512:(ch + 1) * 512],
                                                                  start=(kc == 0), stop=(kc == 7)),
                                         reads=[b_wo, b_mgT[tt // 4]], writes=[bPS[pi]])
                                S.op("dve", lambda e: e.tensor_tensor(out=ht[:, ch * 512:(ch + 1) * 512], in0=PS[pi][:, :],
                                                                      in1=xt[:, ch * 512:(ch + 1) * 512], op=ALU.add),
                                     reads=[bPS[pi], bxt], writes=[bht])
                            stt, bst = stat_ring.next()
                            hb, bhb = hnring.next()
                            S.op("act", lambda e: e.activation(out=hb[:], in_=ht[:], func=AF.Square,
                                                               accum_out=stt[:, 0:1]), reads=[bht], writes=[bhb, bst])
                            S.op("act", lambda e: e.activation(out=stt[:, 1:2], in_=stt[:, 0:1], func=AF.Sqrt,
                                                               scale=1.0 / D, bias=EPS), reads=[bst], writes=[bst])
                            S.op("dve", lambda e: e.reciprocal(out=stt[:, 2:3], in_=stt[:, 1:2]), reads=[bst], writes=[bst])
                            S.op("dve", lambda e: e.scalar_tensor_tensor(out=hb[:], in0=ht[:], scalar=stt[:, 2:3],
                                                                         in1=gmlp[:], op0=ALU.mult, op1=ALU.mult),
                                 reads=[bht, bst, b_gmlp], writes=[bhb])
                            return hb, bhb

                        def hB(tl, hb, bhb):
                            ti = 6 + (tl % 2)
                            for kc in range(8):
                                S.op("pe", lambda e: e.transpose(out=psb(ti)[:, kc * 128:(kc + 1) * 128],
                                                                 in_=hb[:, kc * 128:(kc + 1) * 128], identity=ident[:]),
                                     reads=[bhb, b_ident], writes=[bPS[ti]])
                            S.op("act", lambda e: e.copy(out=hnT[:, :, tl * 128:(tl + 1) * 128],
                                                         in_=psb(ti)[:, 0:1024].rearrange("p (a b) -> p a b", a=8)),
                                 reads=[bPS[ti]], writes=[b_hnT[tl]])

                        prev = None
                        for tl in range(8):
                            cur = hA(tl)
                            if prev is not None:
                                hB(tl - 1, *prev)
                            prev = cur
                        hB(7, *prev)
                        def load_wmi(f8):
                            wm, bwm = wmi_ring.next()
                            S.dma("pool", wm[:], wmi_v[:, :, f8 * 256:(f8 + 1) * 256], writes=[bwm])
                            return wm, bwm

                        def load_wmo(cq):
                            wq, bwq = wmo_ring.next()
                            S.dma("pool", wq[:, 0:16, :], wmo_v[:, 0:16, cq * 256:(cq + 1) * 256], writes=[bwq])
                            S.dma("pool", wq[:, 16:32, :], wmo_v[:, 16:32, cq * 256:(cq + 1) * 256], writes=[bwq])
                            return wq, bwq

                        nxt_wm = load_wmi(0)
                        for f8 in range(16):
                            wm, bwm = nxt_wm
                            if f8 + 1 < 16:
                                nxt_wm = load_wmi(f8 + 1)
                            else:
                                nxt_wq = load_wmo(0)
                            for f2 in range(2):
                                ffc = f8 * 2 + f2
                                for t4 in range(2):
                                    pi = pj.next()
                                    for kc in range(8):
                                        S.op("pe", lambda e: e.matmul(PS[pi][:, :], lhsT=wm[:, kc, f2 * 128:(f2 + 1) * 128],
                                                                      rhs=hnT[:, kc, t4 * 512:(t4 + 1) * 512],
                                                                      start=(kc == 0), stop=(kc == 7)),
                                             reads=[bwm] + b_hnT[t4 * 4:(t4 + 1) * 4], writes=[bPS[pi]])
                                    r_, br = rl_ring.next()
                                    S.op("act", lambda e: e.activation(out=r_[:], in_=PS[pi][:, :], func=AF.Relu),
                                         reads=[bPS[pi]], writes=[br])
                                    S.op("dve", lambda e: e.tensor_tensor(out=AT[:, ffc, t4 * 512:(t4 + 1) * 512],
                                                                          in0=r_[:], in1=r_[:], op=ALU.mult),
                                         reads=[br], writes=[b_AT[ffc][t4]])
                        for cq in range(4):
                            wq, bwq = nxt_wq
                            if cq + 1 < 4:
                                nxt_wq = load_wmo(cq + 1)
                            for tl in range(8):
                                tt = hh * 8 + tl
                                xq_, bxq = xq_ring.next()
                                S.dma("sp", xq_[:], x[s, tt * 128:(tt + 1) * 128, cq * 256:(cq + 1) * 256], writes=[bxq])
                                pi = pj.next()
                                for kc in range(8):
                                    S.op("pe", lambda e: e.matmul(PS[pi][:, 0:256], lhsT=mgT[:, kc, tt * 128:(tt + 1) * 128],
                                                                  rhs=wo[:, kc, cq * 256:(cq + 1) * 256],
                                                                  start=(kc == 0), stop=False),
                                         reads=[b_wo, b_mgT[tt // 4]], writes=[bPS[pi]])
                                for ffc in range(32):
                                    S.op("pe", lambda e: e.matmul(PS[pi][:, 0:256], lhsT=AT[:, ffc, tl * 128:(tl + 1) * 128],
                                                                  rhs=wq[:, ffc, :], start=False, stop=(ffc == 31)),
                                         reads=[bwq, b_AT[ffc][tl // 4]], writes=[bPS[pi]])
                                oq_, boq = oq_ring.next()
                                S.op("dve", lambda e: e.tensor_tensor(out=oq_[:], in0=PS[pi][:, 0:256], in1=xq_[:], op=ALU.add),
                                     reads=[bPS[pi], bxq], writes=[boq])
                                S.dma("sp", out[s, tt * 128:(tt + 1) * 128, cq * 256:(cq + 1) * 256], oq_[:], reads=[boq])
                    S.barrier()
        S.barrier(["sp"])
        print(f"[kernel] instructions ~{S.ninst}, counts {S.cnt}, sbuf peak {peak[0]}")
    return nc


def prep_shared(inp):
    f = lambda a: np.ascontiguousarray(np.asarray(a, dtype=np.float32))
    sh = {}
    sh["w_in"] = f(inp["w_in"][0])
    cw = inp["conv_w"][0]
    tab = np.stack([cw[0], cw[1], cw[2], cw[3], inp["conv_b"][0], inp["gate_a_b"][0].reshape(-1),
                    inp["gate_x_b"][0].reshape(-1), inp["lru_lambda"][0]], axis=-1)
    sh["chtab"] = f(tab.reshape(12, 112, 8).transpose(1, 0, 2))
    sh["gate_a_w"] = f(inp["gate_a_w"][0])
    sh["gate_x_w"] = f(inp["gate_x_w"][0])
    w1k = inp["phi_k_w1"][0].reshape(32, 64, 256).transpose(1, 0, 2)
    w1v = inp["phi_v_w1"][0].reshape(32, 64, 256).transpose(1, 0, 2)
    sh["w1kv"] = f(np.concatenate([w1k, w1v], axis=0))
    sh["peT"] = f(np.concatenate([inp["phi_k_pe"][0].T, inp["phi_v_pe"][0].T], axis=0))
    w2 = np.stack([inp["phi_k_w2"][0].reshape(2, 128, 64), inp["phi_v_w2"][0].reshape(2, 128, 64)], axis=0)
    sh["w2kv"] = f(w2.transpose(2, 0, 1, 3))
    sh["gains"] = f(np.stack([inp["q_norm"][0], inp["kc_norm"][0], inp["ks_norm"][0], inp["kw_norm"][0]], axis=-1))
    sh["rel_bias"] = f(inp["rel_bias"])
    for k in ("proj_a", "proj_b", "w_out", "w_mlp_in", "w_mlp_out"):
        sh[k] = f(inp[k][0])
    sh["norm_mix"] = f(inp["norm_mix"][0].reshape(1, D))
    sh["norm_mlp"] = f(inp["norm_mlp"][0].reshape(1, D))
    sh.update(host_consts())
    return sh


_PROG = {}


def kernel(**inputs):
    n = 8
    sh = prep_shared(inputs)
    x = np.asarray(inputs["x"], dtype=np.float32)
    if "main" not in _PROG:
        _PROG["main"] = build_program(nseq=2)
    nc = _PROG["main"]
    in_maps = []
    for c in range(n):
        m = dict(sh)
        m["x"] = np.ascontiguousarray(x[2 * c:2 * c + 2])
        in_maps.append(m)
    res = run_bass_kernel_spmd(nc, in_maps, core_ids=list(range(n)))
    return np.concatenate([r["out"] for r in res.results], axis=0).astype(np.float32)
```

```python
import os
import math
import numpy as np
import concourse.bass as bass
import concourse.mybir as mybir
from concourse.bass_utils import run_bass_kernel_spmd
from contextlib import ExitStack

F32 = mybir.dt.float32
BF16 = mybir.dt.bfloat16
AF = mybir.ActivationFunctionType
ALU = mybir.AluOpType
AX = mybir.AxisListType

T = 2048
D = 1024
NT = 16
D_RNN = 1344
D_IN = 7344
D_FF = 4096
EPS = 1e-6
NEG = -30000.0
OFF = 160
C_URNN, C_UGATE, C_Q, C_KC, C_VC, C_KS, C_VS, C_KW, C_VW, C_GN, C_GA, C_GB = (
    0, 1344, 2688, 3712, 3968, 4224, 4480, 4736, 4992, 5248, 5296, 6320)


class Buf:
    __slots__ = ("name", "w", "r", "dsem", "dkey", "dcnt", "excl")

    def __init__(self, name):
        self.name = name
        self.excl = False
        self.w = None
        self.r = []
        self.dsem = None
        self.dkey = None
        self.dcnt = 0


class Sched:
    def __init__(self, nc, es):
        self.nc = nc
        self.es = es
        self.engs = {"pe": nc.tensor, "dve": nc.vector, "act": nc.scalar, "pool": nc.gpsimd, "sp": nc.sync}
        self.sem = {k: es.enter_context(nc.semaphore("s_" + k)) for k in self.engs}
        self.cnt = {k: 0 for k in self.engs}
        self.waited = {k: {} for k in self.engs}
        self.nbuf = 0
        self.ninst = 0
        self.dbufs = []
        self.bufs = []
        self.free_dsems = {"sw": [], "hw": []}
        self.ndsem = 0

    def buf(self, name=None):
        self.nbuf += 1
        b = Buf(f"{name or 'b'}{self.nbuf}")
        self.bufs.append(b)
        return b

    def _need(self, eng, dep):
        key, sem, cnt = dep
        if key == eng and eng == "pe":
            return
        w = self.waited[eng]
        if w.get(key, 0) >= cnt:
            return
        self.engs[eng].wait_ge(sem, cnt)
        w[key] = cnt
        self.ninst += 1

    def _deps(self, eng, reads, writes):
        for b in reads:
            if b.w is not None:
                self._need(eng, b.w)
            if b.excl:
                for d in b.r:
                    if d[0] != eng:
                        self._need(eng, d)
        for b in writes:
            if b.w is not None:
                self._need(eng, b.w)
            for d in b.r:
                self._need(eng, d)

    def _mark(self, me, reads, writes):
        for b in writes:
            b.w = me
            b.r = []
        for b in reads:
            if b in writes:
                continue
            b.r.append(me)
            if len(b.r) > 10:
                d = {}
                for k, s, v in b.r:
                    if k not in d or d[k][2] < v:
                        d[k] = (k, s, v)
                b.r = list(d.values())

    def op(self, eng, fn, reads=(), writes=()):
        self._deps(eng, reads, writes)
        inst = fn(self.engs[eng])
        self.cnt[eng] += 1
        inst.then_inc(self.sem[eng], 1)
        self.ninst += 1
        self._mark((eng, self.sem[eng], self.cnt[eng]), reads, writes)
        return inst

    def dma(self, q, out, in_, reads=(), writes=(), own=None, **kw):
        if own is None:
            own = writes[0] if writes else reads[0]
        cls = "sw" if q == "pool" else "hw"
        if own.dsem is None:
            if self.free_dsems[cls]:
                own.dsem, own.dkey, own.dcnt = self.free_dsems[cls].pop()
            else:
                self.ndsem += 1
                own.dkey = f"dsem{cls}{self.ndsem}"
                own.dsem = self.es.enter_context(self.nc.semaphore(own.dkey))
                own.dcnt = 0
            self.dbufs.append(own)
        assert own.dkey.startswith("dsem" + cls), (own.name, own.dkey, q)
        for b in reads:
            if b.w is not None and b.w[0] != own.dkey:
                self._need(q, b.w)
        for b in writes:
            if b.w is not None and b.w[0] != own.dkey:
                self._need(q, b.w)
            for d in b.r:
                if d[0] != own.dkey:
                    self._need(q, d)
        inst = self.engs[q].dma_start(out=out, in_=in_, **kw)
        own.dcnt += 16
        inst.then_inc(own.dsem, 16)
        self.ninst += 1
        self._mark((own.dkey, own.dsem, own.dcnt), reads, writes)
        return inst

    def barrier(self, engines=None):
        full = engines is None
        for e in (engines or list(self.engs)):
            for o in self.engs:
                if o != e and self.cnt[o] > 0:
                    self._need(e, (o, self.sem[o], self.cnt[o]))
            if self.cnt[e] > 0 and e != "pe":
                w = self.waited[e]
                if w.get(e, 0) < self.cnt[e]:
                    self.engs[e].wait_ge(self.sem[e], self.cnt[e])
                    w[e] = self.cnt[e]
            for b in self.dbufs:
                if b.dcnt > 0:
                    self._need(e, (b.dkey, b.dsem, b.dcnt))
        if full:
            for b in self.dbufs:
                self.free_dsems["sw" if b.dkey.startswith("dsemsw") else "hw"].append((b.dsem, b.dkey, b.dcnt))
                b.dsem = None
                b.dkey = None
            self.dbufs = []
            for b in self.bufs:
                b.w = None
                b.r = []


class Ring:
    def __init__(self, items):
        self.items = items
        self.i = 0

    def next(self):
        it = self.items[self.i % len(self.items)]
        self.i += 1
        return it


def _t5_bucket(d):
    if d < 16:
        return d
    v = 16 + int(np.float32(np.log(np.float32(d) / np.float32(16.0))) / np.float32(math.log(8.0)) * np.float32(16.0))
    return min(v, 31)


def _bucket_table(n):
    d = np.arange(n)
    df = np.maximum(d.astype(np.float32), np.float32(1.0))
    large = 16 + (np.log(df / np.float32(16.0)) / np.float32(math.log(128 / 16)) * np.float32(16.0)).astype(np.int32)
    large = np.minimum(large, 31)
    return np.where(d < 16, d, large)


def host_consts():
    c = {}
    c["ident"] = np.eye(128, dtype=np.float32)
    bt = _bucket_table(512)
    oh2 = np.zeros((33, 512), np.float32)
    for i in range(512):
        dist = i - OFF
        if dist < 0:
            oh2[32, i] = NEG
        else:
            oh2[bt[dist], i] += 1.0
            oh2[31, i] -= 1.0
    c["oh2"] = oh2
    b4 = np.zeros((128, 4, 128), np.float32)
    kl = np.arange(128)[:, None]
    ql = np.arange(128)[None, :]
    b4[:] = np.where(ql < kl, 0.0, NEG)[:, None, :]
    c["b4"] = b4.reshape(128, 512)
    selc = np.zeros((17, 16, 128), np.float32)
    for qt in range(16):
        for cc in range(128):
            u = cc - 8 * qt
            if -8 <= u < 8:
                selc[u + 8, qt, cc] = 1.0
            elif u >= 8:
                selc[16, qt, cc] = 1.0
    c["selc"] = selc
    wtf = np.full((1, 2048), NEG, np.float32)
    c["wtf"] = wtf
    cs = np.arange(127) * 16
    ce = cs + 31
    sj = np.arange(32)
    cover = ((cs[:, None] < (sj[None, :] + 1) * 64) & (ce[:, None] >= sj[None, :] * 64)).astype(np.float32)
    c["cover"] = cover
    caus = np.zeros((128, 16, 32), np.float32)
    addc = np.zeros((128, 16, 32), np.float32)
    for qt in range(16):
        for p in range(128):
            qb = (qt * 128 + p) // 64
            for j in range(32):
                cz = j <= qb
                forced = cz and (j == 0 or j >= qb - 1)
                caus[p, qt, j] = 1.0 if cz else 0.0
                addc[p, qt, j] = 1e4 if forced else (0.0 if cz else -1.0)
    c["caus"] = caus
    c["addc"] = addc
    eb = np.zeros((32, 2048), np.float32)
    for j in range(32):
        eb[j, j * 64:(j + 1) * 64] = 1.0
    c["eblk"] = eb
    return c


DBG = {}


def build_program(nseq=2, debug=False, stop_after=None):
    nc = bass.Bass("TRN2", target_bir_lowering=False)
    dram = {}

    def din(name, shape, dt=F32):
        dram[name] = nc.dram_tensor(name, list(shape), dt, kind="ExternalInput").ap()
        return dram[name]

    x = din("x", [nseq, T, D])
    w_in = din("w_in", [D, D_IN])
    chtab_d = din("chtab", [112, 12, 8])
    gaw = din("gate_a_w", [4, 336, 336])
    gxw = din("gate_x_w", [4, 336, 336])
    w1kv_d = din("w1kv", [128, 32, 256])
    peT_d = din("peT", [128, 32])
    w2kv_d = din("w2kv", [128, 2, 2, 64])
    gains_d = din("gains", [64, 4])
    relb_d = din("rel_bias", [32, 16])
    proj_a = din("proj_a", [D_RNN, D])
    proj_b = din("proj_b", [D, D])
    w_out = din("w_out", [D, D])
    w_mi = din("w_mlp_in", [D, D_FF])
    w_mo = din("w_mlp_out", [D_FF, D])
    nmix_d = din("norm_mix", [1, D])
    nmlp_d = din("norm_mlp", [1, D])
    ident_d = din("ident", [128, 128])
    oh2_d = din("oh2", [33, 512])
    b4_d = din("b4", [128, 512])
    selc_d = din("selc", [17, 16, 128])
    wtf_d = din("wtf", [1, 2048])
    cover_d = din("cover", [127, 32])
    caus_d = din("caus", [128, 16, 32])
    addc_d = din("addc", [128, 16, 32])
    eblk_d = din("eblk", [32, 2048])
    out = nc.dram_tensor("out", [nseq, T, D], F32, kind="ExternalOutput").ap()
    mscr = nc.dram_tensor("mscr", [16, 128, 512], BF16, kind="Internal").ap()
    dbg = {}
    if debug:
        for nm, shp in (("d_xnT", [128, 8, T]), ("d_ybT", [128, 8, T]), ("d_yaT", [112, 12, T]),
                        ("d_mgT", [128, 8, T]), ("d_paT", [128, 8, T])):
            dbg[nm] = nc.dram_tensor(nm, shp, BF16, kind="ExternalOutput").ap()

    with ExitStack() as es:
        S = Sched(nc, es)

        ARENA_BYTES = 212480
        arena = es.enter_context(nc.sbuf_tensor("arena", [128, ARENA_BYTES // 2], BF16))
        free_list = [[0, ARENA_BYTES]]
        peak = [0]

        def a_alloc(nb, top=False):
            nb = (nb + 63) // 64 * 64
            if top:
                for i in range(len(free_list) - 1, -1, -1):
                    o, sz = free_list[i]
                    if sz >= nb:
                        if sz == nb:
                            free_list.pop(i)
                        else:
                            free_list[i] = [o, sz - nb]
                        peak[0] = max(peak[0], o + sz)
                        return o + sz - nb, nb
                raise RuntimeError(f"arena out of memory (top): need {nb}, free {free_list}")
            for i, (o, sz) in enumerate(free_list):
                if sz >= nb:
                    if sz == nb:
                        free_list.pop(i)
                    else:
                        free_list[i] = [o + nb, sz - nb]
                    peak[0] = max(peak[0], o + nb)
                    return o, nb
            raise RuntimeError(f"arena out of memory: need {nb}, free {free_list}")

        def a_free(o, nb):
            free_list.append([o, nb])
            free_list.sort()
            i = 0
            while i + 1 < len(free_list):
                if free_list[i][0] + free_list[i][1] == free_list[i + 1][0]:
                    free_list[i][1] += free_list[i + 1][1]
                    free_list.pop(i + 1)
                else:
                    i += 1

        def sb(st, name, shape, dt, top=False):
            shape = list(shape)
            esz = 4 if dt == F32 else 2
            n = 1
            for d_ in shape[1:]:
                n *= d_
            o, nb = a_alloc(n * esz, top)
            st.callback(a_free, o, nb)
            v = arena[0:shape[0], o // 2:o // 2 + n * esz // 2]
            if dt == F32:
                v = v.bitcast(F32)
            if len(shape) == 3:
                v = v.rearrange("p (a b) -> p a b", a=shape[1])
            elif len(shape) == 4:
                v = v.rearrange("p (a b c) -> p a b c", a=shape[1], b=shape[2])
            return v

        PS = [es.enter_context(nc.psum_tensor(f"ps{i}", [128, 512], F32)) for i in range(8)]
        bPS = [S.buf(f"ps{i}") for i in range(8)]
        for b_ in bPS:
            b_.excl = True

        def psb(i):
            return PS[i][:, :].bitcast(BF16)

        ident = sb(es, "ident", [128, 128], BF16); b_ident = S.buf("ident")
        chtab = sb(es, "chtab_s", [112, 12, 8], F32); b_chtab = S.buf("chtab")
        cvec = sb(es, "cvec", [112, 12, 2], F32); b_cvec = S.buf("cvec")
        hbias = sb(es, "hbias", [112, 12, 2], F32); b_hbias = S.buf("hbias")
        gains = sb(es, "gains_s", [64, 4], F32); b_gains = S.buf("gains")
        stat = sb(es, "stat", [128, 192], F32)
        stat_ring = Ring([(stat[:, i * 8:(i + 1) * 8], S.buf("stat")) for i in range(24)])

        S.dma("pool", ident[:], ident_d, writes=[b_ident])
        S.dma("sp", chtab[:], chtab_d, writes=[b_chtab])
        S.dma("sp", gains[:], gains_d, writes=[b_gains])
        S.op("act", lambda e: e.activation(out=cvec[:, :, 0], in_=chtab[:, :, 7], func=AF.Exp, scale=-1.0),
             reads=[b_chtab], writes=[b_cvec])
        S.op("act", lambda e: e.activation(out=cvec[:, :, 0], in_=cvec[:, :, 0], func=AF.Ln, bias=1.0),
             reads=[b_cvec], writes=[b_cvec])
        S.op("dve", lambda e: e.tensor_scalar(out=cvec[:, :, 1], in0=cvec[:, :, 0], scalar1=-4.0, scalar2=None,
                                              op0=ALU.mult), reads=[b_cvec], writes=[b_cvec])
        S.op("dve", lambda e: e.tensor_scalar(out=hbias[:], in0=chtab[:, :, 5:7], scalar1=0.5, scalar2=None,
                                              op0=ALU.mult), reads=[b_chtab], writes=[b_hbias])
        S.op("dve", lambda e: e.tensor_scalar(out=cvec[:, :, 0], in0=cvec[:, :, 0], scalar1=-8.0, scalar2=None,
                                              op0=ALU.mult), reads=[b_cvec], writes=[b_cvec])

        with ExitStack() as st0:
            relb = sb(st0, "relb", [33, 16], F32); b_relb = S.buf("relb")
            rbrep = sb(st0, "rbrep", [33, 16, 128], F32); b_rbrep = S.buf("rbrep")
            oh2 = sb(st0, "oh2_s", [33, 512], F32); b_oh2 = S.buf("oh2")
            mt = sb(st0, "mt", [128, 16, 512], BF16); b_mt = S.buf("mt")
            b_mscr = S.buf("mscr")
            S.op("dve", lambda e: e.memset(relb[:], 1.0), writes=[b_relb])
            S.dma("sp", relb[0:32, :], relb_d, writes=[b_relb])
            S.dma("sp", oh2[:], oh2_d, writes=[b_oh2])
            S.op("dve", lambda e: e.tensor_copy(out=rbrep[:], in_=relb[:, :].unsqueeze(2).to_broadcast([33, 16, 128])),
                 reads=[b_relb], writes=[b_rbrep])
            for h in range(16):
                pi = h % 2
                S.op("pe", lambda e: e.matmul(PS[pi][:, :], lhsT=rbrep[:, h, :], rhs=oh2[:, :], start=True, stop=True),
                     reads=[b_rbrep, b_oh2], writes=[bPS[pi]])
                S.op("act", lambda e: e.copy(out=mt[:, h, :], in_=PS[pi][:, :]), reads=[bPS[pi]], writes=[b_mt])
            S.dma("sp", mscr.rearrange("h p f -> p h f"), mt[:], reads=[b_mt], writes=[b_mscr], own=b_mscr)
            S.barrier()

        for s in range(nseq):
            with ExitStack() as sq:
                sx = ExitStack()
                xnT = sb(sx, f"xnT{s}", [128, 8, T], BF16)
                b_xnT = [S.buf("xnT") for _ in range(NT)]

                def proj_fm(pi, wtile, wcols, M, t4, bw):
                    for kc in range(8):
                        S.op("pe", lambda e: e.matmul(PS[pi][0:M, :], lhsT=wtile[:, kc, wcols],
                                                      rhs=xnT[:, kc, t4 * 512:(t4 + 1) * 512],
                                                      start=(kc == 0), stop=(kc == 7)),
                             reads=[bw] + b_xnT[t4 * 4:(t4 + 1) * 4], writes=[bPS[pi]])

                sy = ExitStack()
                ybT = sb(sy, f"ybT{s}", [128, 8, T], BF16)
                b_ybT = [S.buf("ybT") for _ in range(NT)]

                with ExitStack() as st:
                    w1kv = sb(st, "w1kv_s", [128, 32, 256], BF16); b_w1kv = S.buf("w1kv")
                    w2kv = sb(st, "w2kv_s", [128, 2, 2, 64], BF16); b_w2kv = S.buf("w2kv")
                    peT = sb(st, "peT_s", [128, 32], BF16); b_peT = S.buf("peT")
                    cbias = sb(st, "cbias", [128, 4], F32); b_cbias = S.buf("cbias")
                    B0 = sb(st, "B0", [128, 4, 512], BF16); b_B0 = S.buf("B0")
                    B1 = sb(st, "B1", [128, 4, 512], BF16); b_B1 = S.buf("B1")
                    B4 = sb(st, "B4", [128, 512], BF16); b_B4 = S.buf("B4")
                    WTn = sb(st, "WTn", [17, 4, 512], BF16); b_WTn = S.buf("WTn")
                    SelC = sb(st, "SelC", [17, 16, 128], BF16); b_SelC = S.buf("SelC")
                    caus = sb(st, "caus_s", [128, 16, 32], F32); b_caus = S.buf("caus")
                    addc = sb(st, "addc_s", [128, 16, 32], F32); b_addc = S.buf("addc")
                    VCaug = sb(st, "VCaug", [128, 97], BF16); b_VC = S.buf("VCaug")
                    KcT = sb(st, "KcT", [64, 128], BF16); b_KcT = S.buf("KcT")
                    KsT = sb(st, "KsT", [96, T], BF16); b_KsT = [S.buf("KsT") for _ in range(NT)]
                    KwT = sb(st, "KwT", [64, T], BF16); b_KwT = [S.buf("KwT") for _ in range(NT)]
                    Qaug = sb(st, "Qaug", [96, NT, 4, 128], BF16); b_Q = [S.buf("Q") for _ in range(NT)]
                    Vsw = sb(st, "Vsw", [128, 2, NT, 65], BF16); b_V = [S.buf("V") for _ in range(NT)]
                    GATES = sb(st, "GATES", [128, NT, 48], F32); b_G = [S.buf("G") for _ in range(NT)]
                    KVcT = sb(st, "KVcT", [128, 16, 130], BF16); b_KVc = S.buf("KVcT")
                    HT = sb(st, "HT", [128, 4, 128], BF16); b_HT = S.buf("HT")
                    NWs = [sb(st, f"NW{i}", [128, 96], BF16) for i in range(3)]
                    nw_ring = Ring([(NWs[i], S.buf("NW")) for i in range(3)])
                    Wg = [sb(st, f"Wg{i}", [128, 8, 512], BF16) for i in range(2)]
                    wg_ring = Ring([(Wg[i], S.buf("Wg")) for i in range(2)])
                    Wc = [sb(st, f"Wc{i}", [128, 8, 128], BF16) for i in range(2)]
                    wc_ring = Ring([(Wc[i], S.buf("Wc")) for i in range(2)])
                    Wgn = sb(st, "Wgn", [128, 8, 48], BF16); b_Wgn = S.buf("Wgn")
                    sqt = [sb(st, f"sqt{i}", [128, 384], F32) for i in range(2)]
                    sq_ring = Ring([(sqt[i], S.buf("sqt")) for i in range(2)])
                    nrm = [sb(st, f"nrm{i}", [128, 384], BF16) for i in range(3)]
                    nrm_ring = Ring([(nrm[i], S.buf("nrm")) for i in range(3)])
                    Pt = [sb(st, f"Pt{i}", [128, 512], BF16) for i in range(4)]
                    p_ring = Ring([(Pt[i], S.buf("Pt")) for i in range(4)])
                    Yt = [sb(st, f"Yt{i}", [128, 4, 64], F32) for i in range(2)]
                    y_ring = Ring([(Yt[i], [S.buf("Yt") for _ in range(4)]) for i in range(2)])
                    Ytmp = [sb(st, f"Ytmp{i}", [128, 4, 64], F32) for i in range(2)]
                    ytmp_ring = Ring([(Ytmp[i], S.buf("Ytmp")) for i in range(2)])
                    Yb = [sb(st, f"Yb{i}", [128, 256], BF16) for i in range(2)]
                    yb_ring = Ring([(Yb[i], S.buf("Yb")) for i in range(2)])
                    sm = sb(st, "sm", [128, 16, 64], F32)
                    sm_ring = Ring([(sm[:, i, :], S.buf("sm")) for i in range(16)])
                    kcn = sb(st, "kcn", [128, 64], BF16); b_kcn = S.buf("kcn")
                    b_mscr = S.buf("mscr_r")

                    S.dma("pool", w1kv[:, 0:16, :], w1kv_d[:, 0:16, :], writes=[b_w1kv])
                    S.dma("pool", w1kv[:, 16:32, :], w1kv_d[:, 16:32, :], writes=[b_w1kv])
                    S.dma("pool", w2kv[:], w2kv_d, writes=[b_w2kv])
                    S.dma("pool", peT[:], peT_d, writes=[b_peT])
                    S.dma("pool", B4[:], b4_d, writes=[b_B4])
                    S.dma("pool", SelC[:], selc_d, writes=[b_SelC])
                    b_WTf = S.buf("WTf")
                    S.dma("pool", WTn[16:17, :, :], wtf_d.rearrange("o (a b) -> o a b", a=4), writes=[b_WTf])
                    S.op("pool", lambda e: e.memset(Qaug[:], 0.0), writes=b_Q)
                    S.dma("pool", KsT[64:96, :], eblk_d, writes=b_KsT)
                    S.op("pool", lambda e: e.memset(Vsw[:, :, :, 64:65], 1.0), writes=b_V)
                    S.op("pool", lambda e: e.memset(VCaug[:], 0.0), writes=[b_VC])
                    S.op("pool", lambda e: e.memset(VCaug[:, 64:65], 1.0), writes=[b_VC])
                    S.op("pool", lambda e: e.memset(KVcT[:], 0.0), writes=[b_KVc])
                    S.dma("pool", VCaug[0:127, 65:97], cover_d, writes=[b_VC])
                    for nw_, bnw_ in nw_ring.items:
                        S.op("pool", lambda e: e.memset(nw_[:], 0.0), writes=[bnw_])
                    S.op("pool", lambda e: e.memset(HT[:], 0.0), writes=[b_HT])
                    with ExitStack() as st1:
                        gmix = sb(st1, "gmix", [128, D], F32); b_gmix = S.buf("gmix")
                        S.dma("sp", gmix[:], nmix_d.to_broadcast([128, D]), writes=[b_gmix])
                        xts = [sb(st1, f"xt{i}", [128, D], F32) for i in range(4)]
                        xring = Ring([(xts[i], S.buf("xt")) for i in range(4)])
                        xnbs = [sb(st1, f"xnb{i}", [128, D], BF16) for i in range(3)]
                        xnring = Ring([(xnbs[i], S.buf("xnb")) for i in range(3)])
                        def s1A(tt):
                            xt, bxt = xring.next()
                            S.dma("sp", xt[:], x[s, tt * 128:(tt + 1) * 128, :], writes=[bxt])
                            stt, bst = stat_ring.next()
                            xnb, bxnb = xnring.next()
                            S.op("act", lambda e: e.activation(out=xnb[:], in_=xt[:], func=AF.Square,
                                                               accum_out=stt[:, 0:1]),
                                 reads=[bxt], writes=[bxnb, bst])
                            S.op("act", lambda e: e.activation(out=stt[:, 1:2], in_=stt[:, 0:1], func=AF.Sqrt,
                                                               scale=1.0 / D, bias=EPS), reads=[bst], writes=[bst])
                            S.op("dve", lambda e: e.reciprocal(out=stt[:, 2:3], in_=stt[:, 1:2]), reads=[bst], writes=[bst])
                            S.op("dve", lambda e: e.scalar_tensor_tensor(out=xnb[:], in0=xt[:], scalar=stt[:, 2:3],
                                                                         in1=gmix[:], op0=ALU.mult, op1=ALU.mult),
                                 reads=[bxt, bst, b_gmix], writes=[bxnb])
                            return xnb, bxnb

                        def s1B(tt, xnb, bxnb):
                            pi = 6 + (tt % 2)
                            for kc in range(8):
                                S.op("pe", lambda e: e.transpose(out=psb(pi)[:, kc * 128:(kc + 1) * 128],
                                                                 in_=xnb[:, kc * 128:(kc + 1) * 128], identity=ident[:]),
                                     reads=[bxnb, b_ident], writes=[bPS[pi]])
                            S.op("act", lambda e: e.copy(out=xnT[:, :, tt * 128:(tt + 1) * 128],
                                                         in_=psb(pi)[:, 0:1024].rearrange("p (a b) -> p a b", a=8)),
                                 reads=[bPS[pi]], writes=[b_xnT[tt]])

                        prev = None
                        for tt in range(NT):
                            cur = s1A(tt)
                            if prev is not None:
                                s1B(tt - 1, *prev)
                            prev = cur
                        s1B(NT - 1, *prev)
                        S.barrier()
                    S.dma("sp", caus[:], caus_d, writes=[b_caus])
                    S.dma("sp", addc[:], addc_d, writes=[b_addc])
                    for g in range(4):
                        for r in range(4):
                            h = 4 * g + r
                            base = h * 128 * 512
                            S.dma("sp", B0[:, g, r * 128:(r + 1) * 128],
                                  bass.AP(mscr.tensor, base + OFF, [[511, 128], [1, 128]]), writes=[b_B0], own=b_mscr)
                            S.dma("sp", B1[:, g, r * 128:(r + 1) * 128],
                                  bass.AP(mscr.tensor, base + OFF + 128, [[511, 128], [1, 128]]), writes=[b_B1],
                                  own=b_mscr)
                            S.dma("sp", WTn[0:16, g, r * 128:(r + 1) * 128],
                                  bass.AP(mscr.tensor, base + 257, [[496, 16], [1, 128]]), writes=[b_WTn], own=b_mscr)
                    for kv in range(2):
                        pb = kv * 64
                        for mh in range(2):
                            idx = kv * 2 + mh
                            for l in range(32):
                                S.op("pe", lambda e: e.matmul(PS[kv][:, 2 * mh:2 * mh + 2],
                                                              lhsT=w1kv[pb:pb + 64, l, mh * 128:(mh + 1) * 128],
                                                              rhs=peT[pb:pb + 64, l:l + 1].to_broadcast([64, 2]),
                                                              start=(l == 0), stop=(l == 31)),
                                     reads=[b_w1kv, b_peT], writes=[bPS[kv]])
                    for kv in range(2):
                        S.op("act", lambda e: e.copy(out=cbias[:, 2 * kv:2 * kv + 2], in_=PS[kv][:, 0:4:2]),
                             reads=[bPS[kv]], writes=[b_cbias])
                    S.dma("pool", Wgn[:], w_in.rearrange("(kc p) c -> p kc c", p=128)[:, :, C_GN:C_GN + 48],
                          writes=[b_Wgn])
                    for tt in range(NT):
                        pi = tt % 2
                        for kc in range(8):
                            S.op("pe", lambda e: e.matmul(PS[pi][:, 0:48], lhsT=xnT[:, kc, tt * 128:(tt + 1) * 128],
                                                          rhs=Wgn[:, kc, :], start=(kc == 0), stop=(kc == 7)),
                                 reads=[b_Wgn, b_xnT[tt]], writes=[bPS[pi]])
                        S.op("act", lambda e: e.activation(out=GATES[:, tt, :], in_=PS[pi][:, 0:48], func=AF.Sigmoid),
                             reads=[bPS[pi]], writes=[b_G[tt]])

                    w_in_v = w_in.rearrange("(kc p) c -> p kc c", p=128)

                    def load_group_weights(g):
                        wg, bwg = wg_ring.next()
                        S.dma("pool", wg[:, :, 0:256], w_in_v[:, :, C_Q + 256 * g:C_Q + 256 * (g + 1)], writes=[bwg])
                        for i, c0 in enumerate((C_KS, C_KW, C_VS, C_VW)):
                            S.dma("pool", wg[:, :, 256 + 64 * i:256 + 64 * (i + 1)],
                                  w_in_v[:, :, c0 + 64 * g:c0 + 64 * (g + 1)], writes=[bwg])
                        wc, bwc = wc_ring.next()
                        S.dma("pool", wc[:, :, 0:64], w_in_v[:, :, C_KC + 64 * g:C_KC + 64 * (g + 1)], writes=[bwc])
                        S.dma("pool", wc[:, :, 64:128], w_in_v[:, :, C_VC + 64 * g:C_VC + 64 * (g + 1)], writes=[bwc])
                        return wg, bwg, wc, bwc

                    nxt = load_group_weights(0)
                    for g in range(DBG.get("groups", 4)):
                        wg, bwg, wc, bwc = nxt
                        if g + 1 < 4:
                            nxt = load_group_weights(g + 1)
                        def projA(tt):
                            pi = tt % 2
                            for kc in range(8):
                                S.op("pe", lambda e: e.matmul(PS[pi][:, :], lhsT=xnT[:, kc, tt * 128:(tt + 1) * 128],
                                                              rhs=wg[:, kc, :], start=(kc == 0), stop=(kc == 7)),
                                     reads=[bwg, b_xnT[tt]], writes=[bPS[pi]])
                            sq_, bsq = sq_ring.next()
                            S.op("act", lambda e: e.activation(out=sq_[:], in_=PS[pi][:, 0:384], func=AF.Square),
                                 reads=[bPS[pi]], writes=[bsq])
                            stt, bst = stat_ring.next()
                            S.op("dve", lambda e: e.tensor_reduce(out=stt[:, 0:6],
                                                                  in_=sq_[:, :].rearrange("p (a b) -> p a b", a=6),
                                                                  axis=AX.X, op=ALU.add), reads=[bsq], writes=[bst])
                            stt2, bst2 = stat_ring.next()
                            S.op("act", lambda e: e.activation(out=stt2[:, 0:4], in_=stt[:, 0:4], func=AF.Sqrt,
                                                               scale=1.0, bias=64.0 * EPS), reads=[bst], writes=[bst2])
                            S.op("act", lambda e: e.activation(out=stt2[:, 4:6], in_=stt[:, 4:6], func=AF.Sqrt,
                                                               scale=1.0 / 64, bias=EPS), reads=[bst], writes=[bst2])
                            S.op("dve", lambda e: e.reciprocal(out=stt[:, 0:6], in_=stt2[:, 0:6]),
                                 reads=[bst2], writes=[bst])
                            nr, bnr = nrm_ring.next()
                            S.op("dve", lambda e: e.tensor_tensor(
                                out=nr[:, :].rearrange("p (a b) -> p a b", a=6),
                                in0=PS[pi][:, 0:384].rearrange("p (a b) -> p a b", a=6),
                                in1=stt[:, 0:6].unsqueeze(2).to_broadcast([128, 6, 64]), op=ALU.mult),
                                 reads=[bPS[pi], bst], writes=[bnr])
                            S.op("dve", lambda e: e.tensor_copy(
                                out=Vsw[:, :, tt, 0:64],
                                in_=PS[pi][:, 384:512].rearrange("p (a b) -> p a b", a=2)),
                                 reads=[bPS[pi]], writes=[b_V[tt]])
                            return nr, bnr

                        def projB(tt, nr, bnr):
                            ti = 6 + (tt % 2)
                            for i in range(6):
                                S.op("pe", lambda e: e.transpose(out=psb(ti)[0:64, i * 128:(i + 1) * 128],
                                                                 in_=nr[:, i * 64:(i + 1) * 64], identity=ident[:]),
                                     reads=[bnr, b_ident], writes=[bPS[ti]])
                            S.op("act", lambda e: e.activation(
                                out=Qaug[0:64, tt], in_=psb(ti)[0:64, 0:512].rearrange("p (a b) -> p a b", a=4),
                                func=AF.Copy, scale=gains[:, 0:1]), reads=[bPS[ti], b_gains], writes=[b_Q[tt]])
                            S.op("dve", lambda e: e.tensor_scalar(
                                out=KsT[0:64, tt * 128:(tt + 1) * 128], in0=psb(ti)[0:64, 512:640],
                                scalar1=gains[:, 2:3], scalar2=None, op0=ALU.mult),
                                 reads=[bPS[ti], b_gains], writes=[b_KsT[tt]])
                            S.op("dve", lambda e: e.tensor_scalar(
                                out=KwT[0:64, tt * 128:(tt + 1) * 128], in0=psb(ti)[0:64, 640:768],
                                scalar1=gains[:, 3:4], scalar2=None, op0=ALU.mult),
                                 reads=[bPS[ti], b_gains], writes=[b_KwT[tt]])

                        prevA = None
                        for tt in range(NT):
                            curA = projA(tt)
                            if prevA is not None:
                                projB(tt - 1, *prevA)
                            prevA = curA
                        projB(NT - 1, *prevA)
                        if DBG.get("skip_phi"):
                            continue
                        for t4 in range(4):
                            pi = t4 % 2
                            proj_fm(pi, wc, slice(0, 128), 128, t4, bwc)
                            S.op("act", lambda e: e.copy(out=KVcT[:, :, t4 * 32:(t4 + 1) * 32],
                                                         in_=PS[pi][:, :].rearrange("p (c s) -> p s c", s=16)),
                                 reads=[bPS[pi]], writes=[b_KVc])
                        if DBG.get("phi_stop", 99) <= 1:
                            continue
                        for kv in range(2):
                            pb = kv * 64
                            for mh in range(2):
                                idx = kv * 2 + mh
                                for l in range(32):
                                    S.op("pe", lambda e: e.matmul(
                                        PS[kv][:, mh * 128:(mh + 1) * 128],
                                        lhsT=w1kv[pb:pb + 64, l, mh * 128:(mh + 1) * 128],
                                        rhs=(KVcT[pb:pb + 64, l, 0:128] if l < 16 else KVcT[pb:pb + 64, l - 16, 1:129]),
                                        start=(l == 0), stop=(l == 31)),
                                         reads=[b_w1kv, b_KVc], writes=[bPS[kv]])
                        if DBG.get("phi_stop", 99) <= 2:
                            continue
                        for idx in range(4):
                            S.op("act", lambda e: e.activation(out=HT[:, idx, :],
                                                               in_=PS[idx // 2][:, (idx % 2) * 128:(idx % 2 + 1) * 128],
                                                               func=AF.Gelu_apprx_tanh, bias=cbias[:, idx:idx + 1]),
                                 reads=[bPS[idx // 2], b_cbias], writes=[b_HT])
                        if DBG.get("phi_stop", 99) <= 3:
                            continue
                        for kv in range(2):
                            for mh in range(2):
                                S.op("pe", lambda e: e.matmul(PS[2][:, kv * 64:(kv + 1) * 64],
                                                              lhsT=HT[:, kv * 2 + mh, :], rhs=w2kv[:, kv, mh, :],
                                                              start=(mh == 0), stop=(mh == 1)),
                                     reads=[b_HT, b_w2kv], writes=[bPS[2]])
                        if DBG.get("phi_stop", 99) <= 4:
                            continue
                        stt, bst = stat_ring.next()
                        sq_, bsq = sq_ring.next()
                        S.op("act", lambda e: e.activation(out=sq_[:, 0:64], in_=PS[2][:, 0:64], func=AF.Square,
                                                           accum_out=stt[:, 0:1]), reads=[bPS[2]], writes=[bsq, bst])
                        S.op("act", lambda e: e.activation(out=stt[:, 1:2], in_=stt[:, 0:1], func=AF.Sqrt,
                                                           scale=1.0 / 64, bias=EPS), reads=[bst], writes=[bst])
                        S.op("dve", lambda e: e.reciprocal(out=stt[:, 2:3], in_=stt[:, 1:2]),
                             reads=[bst], writes=[bst])
                        S.op("dve", lambda e: e.tensor_scalar(out=kcn[:, :], in0=PS[2][:, 0:64],
                                                              scalar1=stt[:, 2:3], scalar2=None, op0=ALU.mult),
                             reads=[bPS[2], bst], writes=[b_kcn])
                        S.op("act", lambda e: e.copy(out=VCaug[:, 0:64], in_=PS[2][:, 64:128]),
                             reads=[bPS[2]], writes=[b_VC])
                        if DBG.get("phi_stop", 99) <= 5:
                            continue
                        S.op("pe", lambda e: e.transpose(out=psb(6)[0:64, 0:128], in_=kcn[:, :],
                                                         identity=ident[:]),
                             reads=[b_kcn, b_ident], writes=[bPS[6]])
                        S.op("act", lambda e: e.activation(out=KcT[:, 0:128], in_=psb(6)[0:64, 0:128], func=AF.Copy,
                                                           scale=gains[:, 1:2]), reads=[bPS[6], b_gains], writes=[b_KcT])

                        OcV = PS[2][:, 0:388].rearrange("p (r c) -> p r c", r=4)
                        OsV = PS[3][:, 0:260].rearrange("p (r c) -> p r c", r=4)
                        OwV = PS[4][:, 0:260].rearrange("p (r c) -> p r c", r=4)
                        NQ = DBG.get("qts", NT)
                        items = []
                        for qq in range(NQ + 1):
                            if qq < NQ:
                                items.append(("c", qq, 0, True, True))
                                k0 = max(0, qq - 4)
                                items += [("w", qq, kt, kt == k0, kt == qq) for kt in range(k0, qq + 1)]
                            if qq >= 1:
                                items += [("s", qq - 1, kt, kt == 0, kt == qq - 1) for kt in range(0, qq)]
                        qst = {}
                        deferred_tail = []

                        def qstate(qt):
                            if qt not in qst:
                                Y, bY = y_ring.next()
                                yb, byb = yb_ring.next()
                                nw, bnw = nw_ring.next()
                                qst[qt] = dict(
                                    Y=Y, bY=bY, yb=yb, byb=byb, nw=nw, bnw=bnw,
                                    Q64=Qaug[0:64, qt].rearrange("p a b -> p (a b)"),
                                    Q96=Qaug[0:96, qt].rearrange("p a b -> p (a b)"),
                                    gat=GATES[:, qt, :].rearrange("p (h j) -> p h j", j=3))
                            return qst[qt]

                        def emit_qk(kind, qt, kt, si):
                            q = qstate(qt)
                            if kind == "c":
                                S.op("pe", lambda e: e.matmul(PS[si][:, :], lhsT=KcT[:, 0:128], rhs=q["Q64"],
                                                              start=True, stop=False),
                                     reads=[b_KcT, b_Q[qt]], writes=[bPS[si]])
                                S.op("pe", lambda e: e.matmul(PS[si][:, :], lhsT=SelC[:, qt, :],
                                                              rhs=WTn[:, g, :], start=False, stop=True),
                                     reads=[b_SelC, b_WTn, b_WTf], writes=[bPS[si]])
                                return
                            dl = qt - kt
                            if kind == "w":
                                bt = B0[:, g, :] if dl == 0 else B1[:, g, :] if dl == 1 else B4[:, :] if dl == 4 else None
                                bb = b_B0 if dl == 0 else b_B1 if dl == 1 else b_B4
                                S.op("pe", lambda e: e.matmul(PS[si][:, :], lhsT=KwT[:, kt * 128:(kt + 1) * 128],
                                                              rhs=q["Q64"], start=True, stop=(bt is None)),
                                     reads=[b_KwT[kt], b_Q[qt]], writes=[bPS[si]])
                            else:
                                bt = B0[:, g, :] if dl == 0 else B1[:, g, :] if dl == 1 else None
                                bb = b_B0 if dl == 0 else b_B1
                                S.op("pe", lambda e: e.matmul(PS[si][:, :], lhsT=KsT[:, kt * 128:(kt + 1) * 128],
                                                              rhs=q["Q96"], start=True, stop=(bt is None)),
                                     reads=[b_KsT[kt], b_Q[qt]], writes=[bPS[si]])
                            if bt is not None:
                                S.op("pe", lambda e: e.matmul(PS[si][:, :], lhsT=ident[:], rhs=bt,
                                                              start=False, stop=True),
                                     reads=[b_ident, bb], writes=[bPS[si]])

                        def emit_exp(si):
                            P, bP = p_ring.next()
                            S.op("act", lambda e: e.activation(out=P[:, :], in_=PS[si][:, :], func=AF.Exp),
                                 reads=[bPS[si]], writes=[bP])
                            return P, bP

                        def finish_branch(j, qt, OV, oi):
                            q = qstate(qt)
                            Y, bY, yb, byb, gat = q["Y"], q["bY"], q["yb"], q["byb"], q["gat"]
                            stt, bst = stat_ring.next()
                            if j == 0:
                                S.op("dve", lambda e: e.tensor_scalar(out=stt[:, 0:4], in0=OV[:, :, 64], scalar1=1e-30,
                                                                      scalar2=None, op0=ALU.max),
                                     reads=[bPS[oi]], writes=[bst])
                                S.op("dve", lambda e: e.reciprocal(out=stt[:, 4:8], in_=stt[:, 0:4]),
                                     reads=[bst], writes=[bst])
                            else:
                                S.op("dve", lambda e: e.reciprocal(out=stt[:, 4:8], in_=OV[:, :, 64]),
                                     reads=[bPS[oi]], writes=[bst])
                            stg, bsg = stat_ring.next()
                            S.op("dve", lambda e: e.tensor_tensor(out=stg[:, 0:4], in0=stt[:, 4:8],
                                                                  in1=gat[:, 4 * g:4 * g + 4, j], op=ALU.mult),
                                 reads=[bst, b_G[qt]], writes=[bsg])
                            for r in range(4):
                                if j == 0:
                                    S.op("dve", lambda e: e.tensor_scalar(out=Y[:, r, :], in0=OV[:, r, 0:64],
                                                                          scalar1=stg[:, r:r + 1], scalar2=None,
                                                                          op0=ALU.mult),
                                         reads=[bPS[oi], bsg], writes=[bY[r]])
                                elif j == 2:
                                    S.op("dve", lambda e: e.scalar_tensor_tensor(
                                        out=Y[:, r, :], in0=OV[:, r, 0:64], scalar=stg[:, r:r + 1], in1=Y[:, r, :],
                                        op0=ALU.mult, op1=ALU.add),
                                         reads=[bPS[oi], bsg, bY[r]], writes=[bY[r]])
                                else:
                                    S.op("dve", lambda e: e.scalar_tensor_tensor(
                                        out=yb[:, r * 64:(r + 1) * 64], in0=OV[:, r, 0:64], scalar=stg[:, r:r + 1],
                                        in1=Y[:, r, :], op0=ALU.mult, op1=ALU.add),
                                         reads=[bPS[oi], bsg, bY[r]], writes=[byb])
                            return stt, bst

                        def emit_pv(kind, qt, kt, first, last, P, bP):
                            q = qstate(qt)
                            if kind == "c":
                                for r in range(4):
                                    S.op("pe", lambda e: e.matmul(PS[2][:, r * 97:(r + 1) * 97],
                                                                  lhsT=P[:, r * 128:(r + 1) * 128],
                                                                  rhs=VCaug[:, :], start=True, stop=True),
                                         reads=[bP, b_VC], writes=[bPS[2]])
                                stt, bst = finish_branch(0, qt, OcV, 2)
                                imp, bimp = sm_ring.next()
                                for r in range(4):
                                    if r == 0:
                                        S.op("dve", lambda e: e.tensor_scalar(out=imp[:, 0:32], in0=OcV[:, 0, 65:97],
                                                                              scalar1=stt[:, 4:5], scalar2=None,
                                                                              op0=ALU.mult),
                                             reads=[bPS[2], bst], writes=[bimp])
                                    else:
                                        S.op("dve", lambda e: e.scalar_tensor_tensor(
                                            out=imp[:, 0:32], in0=OcV[:, r, 65:97], scalar=stt[:, 4 + r:5 + r],
                                            in1=imp[:, 0:32], op0=ALU.mult, op1=ALU.add),
                                             reads=[bPS[2], bst, bimp], writes=[bimp])
                                S.op("dve", lambda e: e.tensor_tensor(out=imp[:, 0:32], in0=imp[:, 0:32],
                                                                      in1=caus[:, qt, :], op=ALU.mult),
                                     reads=[bimp, b_caus], writes=[bimp])
                                S.op("dve", lambda e: e.tensor_tensor(out=imp[:, 0:32], in0=imp[:, 0:32],
                                                                      in1=addc[:, qt, :], op=ALU.add),
                                     reads=[bimp, b_addc], writes=[bimp])
                                m8, bm8 = stat_ring.next()
                                S.op("dve", lambda e: e.max(out=m8[:, 0:8], in_=imp[:, 0:32]), reads=[bimp], writes=[bm8])
                                S.op("dve", lambda e: e.match_replace(out=imp[:, 32:64], in_to_replace=m8[:, 0:8],
                                                                      in_values=imp[:, 0:32], imm_value=-2.0),
                                     reads=[bimp, bm8], writes=[bimp])
                                m8b, bm8b = stat_ring.next()
                                S.op("dve", lambda e: e.max(out=m8b[:, 0:8], in_=imp[:, 32:64]), reads=[bimp],
                                     writes=[bm8b])
                                S.op("dve", lambda e: e.tensor_scalar(out=m8[:, 0:1], in0=m8b[:, 7:8], scalar1=0.0,
                                                                      scalar2=None, op0=ALU.max),
                                     reads=[bm8b], writes=[bm8])
                                S.op("dve", lambda e: e.tensor_scalar(out=q["nw"][:, 64:96], in0=imp[:, 0:32],
                                                                      scalar1=m8[:, 0:1], scalar2=1.0,
                                                                      op0=ALU.is_ge, op1=ALU.subtract),
                                     reads=[bimp, bm8], writes=[q["bnw"]])
                                q["c_done"] = True
                                return
                            oi = 3 if kind == "s" else 4
                            vi = 0 if kind == "s" else 1
                            for r in range(4):
                                S.op("pe", lambda e: e.matmul(PS[oi][:, r * 65:(r + 1) * 65],
                                                              lhsT=P[:, r * 128:(r + 1) * 128], rhs=Vsw[:, vi, kt, :],
                                                              start=(first and r == 0), stop=last, skip_group_check=True),
                                     reads=[bP, b_V[kt]], writes=[bPS[oi]])
                            if last:
                                finish_branch(1 if kind == "s" else 2, qt, OsV if kind == "s" else OwV, oi)
                                if kind == "s":
                                    def make_tail(yb=q["yb"], byb=q["byb"], g=g, qt=qt):
                                        def tail():
                                            for i in range(2):
                                                S.op("pe", lambda e: e.transpose(out=psb(6)[:, i * 128:(i + 1) * 128],
                                                                                 in_=yb[:, i * 128:(i + 1) * 128],
                                                                                 identity=ident[:]),
                                                     reads=[byb, b_ident], writes=[bPS[6]])
                                            S.op("dve", lambda e: e.tensor_copy(
                                                out=ybT[:, 2 * g:2 * g + 2, qt * 128:(qt + 1) * 128],
                                                in_=psb(6)[:, 0:256].rearrange("p (a b) -> p a b", a=2)),
                                                 reads=[bPS[6]], writes=[b_ybT[qt]])
                                        return tail
                                    deferred_tail.append(make_tail())
                                    del qst[qt]

                        def emit_mask(qt):
                            q = qstate(qt)
                            S.op("pe", lambda e: e.transpose(out=psb(7)[0:96, 0:128], in_=q["nw"][:, 0:96],
                                                             identity=ident[:]),
                                 reads=[q["bnw"], b_ident], writes=[bPS[7]])
                            S.op("dve", lambda e: e.tensor_scalar(
                                out=Qaug[64:96, qt],
                                in0=psb(7)[64:96, 0:128].unsqueeze(1).to_broadcast([32, 4, 128]), scalar1=-NEG,
                                scalar2=None, op0=ALU.mult),
                                 reads=[bPS[7]], writes=[b_Q[qt]])

                        def need_mask(mq):
                            if mq < NQ and mq not in mask_done:
                                while any(p[0] == "c" and p[1] == mq for p in pend):
                                    emit_pv(*pend.pop(0))
                                emit_mask(mq)
                                mask_done.add(mq)

                        pend = []
                        mask_done = set()
                        sbanks = [5, 1, 0]
                        DEPTH = 2
                        for idx, (kind, qt, kt, first, last) in enumerate(items):
                            if kind == "s" and first:
                                need_mask(qt)
                                while deferred_tail:
                                    deferred_tail.pop(0)()
                            si = sbanks[idx % 3]
                            emit_qk(kind, qt, kt, si)
                            P, bP = emit_exp(si)
                            pend.append((kind, qt, kt, first, last, P, bP))
                            if len(pend) > DEPTH:
                                emit_pv(*pend.pop(0))
                            if kind == "s" and last:
                                need_mask(qt + 1)
                        while pend:
                            emit_pv(*pend.pop(0))
                        while deferred_tail:
                            deferred_tail.pop(0)()
                    S.barrier()
                if debug:
                    S.dma("sp", dbg["d_ybT"], ybT[:], reads=b_ybT, own=b_ybT[0])
                if stop_after == "nsa":
                    S.barrier()
                    sx.close()
                    sy.close()
                    continue

                mgT = sb(sq, f"mgT{s}", [128, 8, T], BF16)
                b_mgT = [S.buf("mgT") for _ in range(4)]
                with ExitStack() as st2:
                    Wp = [sb(st2, f"Wpb{i}", [128, 8, 256], BF16) for i in range(2)]
                    Wgb = [sb(st2, f"Wgb{i}", [128, 8, 256], BF16) for i in range(2)]
                    wp_ring = Ring([(Wp[i], S.buf("Wpb"), Wgb[i], S.buf("Wgb")) for i in range(2)])
                    sg = [sb(st2, f"sgb{i}", [128, 512], F32) for i in range(2)]
                    sg_ring = Ring([(sg[i], S.buf("sgb")) for i in range(2)])
                    tmpb = [sb(st2, f"tmpb{i}", [128, 512], F32) for i in range(2)]
                    tmp_ring = Ring([(tmpb[i], S.buf("tmpb")) for i in range(2)])
                    pj = Ring([0, 1, 2, 3, 4, 5])
                    w_in_v = w_in.rearrange("(kc p) c -> p kc c", p=128)
                    pb_v = proj_b.rearrange("(kc p) o -> p kc o", p=128)

                    def load_pb(c4):
                        wp, bwp, wgb, bwgb = wp_ring.next()
                        S.dma("pool", wp[:], pb_v[:, :, c4 * 256:(c4 + 1) * 256], writes=[bwp])
                        S.dma("pool", wgb[:], w_in_v[:, :, C_GB + c4 * 256:C_GB + (c4 + 1) * 256], writes=[bwgb])
                        return wp, bwp, wgb, bwgb

                    nxt = load_pb(0)
                    for c4 in range(4):
                        wp, bwp, wgb, bwgb = nxt
                        if c4 + 1 < 4:
                            nxt = load_pb(c4 + 1)
                        for f2 in range(2):
                            fc = c4 * 2 + f2
                            for t4 in range(4):
                                pg = pj.next()
                                proj_fm(pg, wgb, slice(f2 * 128, (f2 + 1) * 128), 128, t4, bwgb)
                                sg_, bsg = sg_ring.next()
                                S.op("act", lambda e: e.activation(out=sg_[:], in_=PS[pg][:, :], func=AF.Sigmoid),
                                     reads=[bPS[pg]], writes=[bsg])
                                pp = pj.next()
                                for kc in range(8):
                                    S.op("pe", lambda e: e.matmul(PS[pp][:, :], lhsT=wp[:, kc, f2 * 128:(f2 + 1) * 128],
                                                                  rhs=ybT[:, kc, t4 * 512:(t4 + 1) * 512],
                                                                  start=(kc == 0), stop=(kc == 7)),
                                         reads=[bwp] + b_ybT[t4 * 4:(t4 + 1) * 4], writes=[bPS[pp]])
                                S.op("dve", lambda e: e.tensor_tensor(out=mgT[:, fc, t4 * 512:(t4 + 1) * 512],
                                                                      in0=PS[pp][:, :], in1=sg_[:], op=ALU.mult),
                                     reads=[bPS[pp], bsg], writes=[b_mgT[t4]])
                    S.barrier()
                if debug:
                    S.dma("sp", dbg["d_paT"], mgT[:], reads=b_mgT, own=b_mgT[0])
                    S.barrier()
                sy.close()
                if stop_after == "merge":
                    sx.close()
                    continue

                with ExitStack() as st:
                    yaT = sb(st, "yaT", [112, 12, T], BF16)
                    b_yaT = [[S.buf("yaT") for _ in range(4)] for _ in range(12)]
                    with ExitStack() as st2:
                        QW = 512
                        Wu = [sb(st2, f"Wu{i}", [128, 8, 336], BF16) for i in range(2)]
                        Wv = [sb(st2, f"Wv{i}", [128, 8, 336], BF16) for i in range(2)]
                        wu_ring = Ring([(Wu[i], S.buf("Wu"), Wv[i], S.buf("Wv")) for i in range(2)])
                        Wa = sb(st2, "Wa", [112, 3, 336], BF16); bWa = S.buf("Wa")
                        Wx = sb(st2, "Wx", [112, 3, 336], BF16); bWx = S.buf("Wx")
                        upad = [sb(st2, f"upad{j}", [112, 3 + QW], F32) for j in range(3)]
                        b_upad = [S.buf("upad") for _ in range(3)]
                        csets = []
                        for i in range(2):
                            csets.append(dict(
                                xc=[sb(st2, f"xc{i}{j}", [112, QW], F32) for j in range(3)],
                                b_xc=[S.buf("xc") for _ in range(3)],
                                xcb=[sb(st2, f"xcb{i}{j}", [112, QW], BF16) for j in range(3)],
                                b_xcb=[S.buf("xcb") for _ in range(3)]))
                        rsets = []
                        for i in range(2):
                            rsets.append(dict(
                                R=[sb(st2, f"Rt{i}{j}", [112, QW], F32) for j in range(3)],
                                I=[sb(st2, f"It{i}{j}", [112, QW], F32) for j in range(3)],
                                M=[sb(st2, f"Mt{i}{j}", [112, QW], F32) for j in range(3)],
                                bR=[S.buf("Rt") for _ in range(3)],
                                bI=[S.buf("It") for _ in range(3)],
                                bM=[S.buf("Mt") for _ in range(3)]))
                        hcar = sb(st2, "hcar", [112, 12], F32); b_hcar = [S.buf("hcar") for _ in range(12)]
                        w_in_v = w_in.rearrange("(kc p) c -> p kc c", p=128)
                        blkw = {}

                        def load_uv(n):
                            wu, bwu, wv, bwv = wu_ring.next()
                            S.dma("pool", wu[:], w_in_v[:, :, C_URNN + 336 * n:C_URNN + 336 * (n + 1)], writes=[bwu])
                            S.dma("pool", wv[:], w_in_v[:, :, C_UGATE + 336 * n:C_UGATE + 336 * (n + 1)], writes=[bwv])
                            blkw[n] = (wu, bwu, wv, bwv)

                        def load_ax(n):
                            S.dma("pool", Wa[:], gaw[n].rearrange("(i p) o -> p i o", p=112), writes=[bWa])
                            S.dma("pool", Wx[:], gxw[n].rearrange("(i p) o -> p i o", p=112), writes=[bWx])

                        pj = Ring([0, 1, 2, 3, 4, 5])

                        def conv_stage(k):
                            n, th = divmod(k, 4)
                            cs = csets[k % 2]
                            wu, bwu = blkw[n][0], blkw[n][1]
                            for j in range(3):
                                up, bup = upad[j], b_upad[j]
                                if th == 0:
                                    S.op("pool", lambda e: e.memset(up[:, 0:3], 0.0), writes=[bup])
                                else:
                                    S.op("act", lambda e: e.copy(out=up[:, 0:3], in_=up[:, QW:QW + 3]),
                                         reads=[bup], writes=[bup])
                                pi = pj.next()
                                proj_fm(pi, wu, slice(112 * j, 112 * (j + 1)), 112, th, bwu)
                                S.op("act", lambda e: e.copy(out=up[:, 3:3 + QW], in_=PS[pi][0:112, :]),
                                     reads=[bPS[pi]], writes=[bup])
                            for j in range(3):
                                ci = 3 * n + j
                                up, bup = upad[j], b_upad[j]
                                xc_, bxc = cs["xc"][j], cs["b_xc"][j]
                                S.op("dve", lambda e: e.tensor_scalar(out=xc_[:], in0=up[:, 3:3 + QW],
                                                                      scalar1=chtab[:, ci, 3:4], scalar2=chtab[:, ci, 4:5],
                                                                      op0=ALU.mult, op1=ALU.add),
                                     reads=[bup, b_chtab], writes=[bxc])
                                for kk in range(3):
                                    S.op("dve", lambda e: e.scalar_tensor_tensor(
                                        out=xc_[:], in0=up[:, kk:kk + QW], scalar=chtab[:, ci, kk:kk + 1],
                                        in1=xc_[:], op0=ALU.mult, op1=ALU.add),
                                         reads=[bup, b_chtab, bxc], writes=[bxc])

                        def conv_cast(k):
                            cs = csets[k % 2]
                            for j in range(3):
                                S.op("act", lambda e: e.copy(out=cs["xcb"][j][:], in_=cs["xc"][j][:]),
                                     reads=[cs["b_xc"][j]], writes=[cs["b_xcb"][j]])

                        def mainA(k):
                            n, th = divmod(k, 4)
                            cs = csets[k % 2]
                            rs = rsets[k % 2]
                            Rt, It, Mt, bR, bI, bM = rs["R"], rs["I"], rs["M"], rs["bR"], rs["bI"], rs["bM"]
                            for j in range(3):
                                ci = 3 * n + j
                                for (wt, bwt, dst, bdst, col) in ((Wa, bWa, Rt[j], bR[j], 0), (Wx, bWx, It[j], bI[j], 1)):
                                    pr = pj.next()
                                    for i in range(3):
                                        S.op("pe", lambda e: e.matmul(PS[pr][0:112, :],
                                                                      lhsT=wt[:, i, 112 * j:112 * (j + 1)],
                                                                      rhs=cs["xcb"][i][:, :], start=(i == 0), stop=(i == 2)),
                                             reads=[bwt, cs["b_xcb"][i]], writes=[bPS[pr]])
                                    S.op("act", lambda e: e.activation(out=dst[:], in_=PS[pr][0:112, :], func=AF.Tanh,
                                                                       bias=hbias[:, ci, col:col + 1], scale=0.5),
                                         reads=[bPS[pr], b_hbias], writes=[bdst])
                            for j in range(3):
                                ci = 3 * n + j
                                S.op("act", lambda e: e.activation(out=Rt[j][:], in_=Rt[j][:], func=AF.Exp,
                                                                   scale=cvec[:, ci, 1:2], bias=cvec[:, ci, 1:2]),
                                     reads=[bR[j], b_cvec], writes=[bR[j]])
                                S.op("pool", lambda e: e.tensor_tensor(out=Mt[j][:], in0=Rt[j][:], in1=Rt[j][:], op=ALU.mult),
                                     reads=[bR[j]], writes=[bM[j]])
                            for j in range(3):
                                S.op("act", lambda e: e.activation(out=Mt[j][:], in_=Mt[j][:], func=AF.Sqrt,
                                                                   scale=-0.25, bias=0.25),
                                     reads=[bM[j]], writes=[bM[j]])
                            for j in range(3):
                                S.op("dve", lambda e: e.scalar_tensor_tensor(out=It[j][:], in0=It[j][:], scalar=1.0,
                                                                             in1=cs["xc"][j][:], op0=ALU.add, op1=ALU.mult),
                                     reads=[bI[j], cs["b_xc"][j]], writes=[bI[j]])
                                if th == 0:
                                    S.op("dve", lambda e: e.memset(Mt[j][:, 0:1], 0.5), writes=[bM[j]])
                            for j in range(3):
                                S.op("pool", lambda e: e.tensor_tensor(out=Mt[j][:], in0=Mt[j][:], in1=It[j][:], op=ALU.mult),
                                     reads=[bI[j], bM[j]], writes=[bM[j]])
                            for j in range(3):
                                ci = 3 * n + j
                                S.op("dve", lambda e: e.tensor_tensor_scan(
                                    out=It[j][:], data0=Rt[j][:], data1=Mt[j][:],
                                    initial=(0.0 if th == 0 else hcar[:, ci:ci + 1]), op0=ALU.mult, op1=ALU.add),
                                     reads=[bR[j], bM[j], b_hcar[ci]], writes=[bI[j]])
                                if th < 3:
                                    S.op("dve", lambda e: e.tensor_copy(out=hcar[:, ci:ci + 1], in_=It[j][:, QW - 1:QW]),
                                         reads=[bI[j]], writes=[b_hcar[ci]])

                        def mainB(k):
                            n, th = divmod(k, 4)
                            rs = rsets[k % 2]
                            Rt, It, bR, bI = rs["R"], rs["I"], rs["bR"], rs["bI"]
                            wv, bwv = blkw[n][2], blkw[n][3]
                            for j in range(3):
                                pg = pj.next()
                                proj_fm(pg, wv, slice(112 * j, 112 * (j + 1)), 112, th, bwv)
                                S.op("act", lambda e: e.activation(out=Rt[j][:], in_=PS[pg][0:112, :],
                                                                   func=AF.Gelu_apprx_tanh),
                                     reads=[bPS[pg]], writes=[bR[j]])
                            for j in range(3):
                                ci = 3 * n + j
                                S.op("dve", lambda e: e.tensor_tensor(out=yaT[:, ci, th * QW:(th + 1) * QW], in0=It[j][:],
                                                                      in1=Rt[j][:], op=ALU.mult),
                                     reads=[bR[j], bI[j]], writes=[b_yaT[ci][th]])

                        load_uv(0)
                        load_ax(0)
                        conv_stage(0)
                        conv_cast(0)
                        for k in range(16):
                            n, th = divmod(k, 4)
                            if k + 1 < 16:
                                conv_stage(k + 1)
                            mainA(k)
                            if k + 1 < 16:
                                conv_cast(k + 1)
                            if th == 3 and n + 1 < 4:
                                load_ax(n + 1)
                            if k >= 1:
                                mainB(k - 1)
                            if th == 0 and n + 1 < 4:
                                load_uv(n + 1)
                        mainB(15)
                        S.barrier()
                    if debug:
                        S.dma("sp", dbg["d_yaT"], yaT[:], reads=[b for l in b_yaT for b in l], own=b_yaT[0][0])
                    with ExitStack() as st2:
                        Wp = [sb(st2, f"Wp{i}", [112, 12, 256], BF16) for i in range(2)]
                        Wga = [sb(st2, f"Wga{i}", [128, 8, 256], BF16) for i in range(2)]
                        wp_ring = Ring([(Wp[i], S.buf("Wp"), Wga[i], S.buf("Wga")) for i in range(2)])
                        sg = [sb(st2, f"sg{i}", [128, 512], F32) for i in range(2)]
                        sg_ring = Ring([(sg[i], S.buf("sg")) for i in range(2)])
                        tmpa = [sb(st2, f"tmpa{i}", [128, 512], F32) for i in range(2)]
                        tmp_ring = Ring([(tmpa[i], S.buf("tmpa")) for i in range(2)])
                        pj = Ring([0, 1, 2, 3, 4, 5])

                        def load_pa(c4):
                            wp, bwp, wga, bwga = wp_ring.next()
                            S.dma("pool", wp[:], proj_a.rearrange("(i p) o -> p i o", p=112)[:, :, c4 * 256:(c4 + 1) * 256],
                                  writes=[bwp])
                            S.dma("pool", wga[:], w_in_v[:, :, C_GA + c4 * 256:C_GA + (c4 + 1) * 256], writes=[bwga])
                            return wp, bwp, wga, bwga

                        nxt = load_pa(0)
                        for c4 in range(4):
                            wp, bwp, wga, bwga = nxt
                            if c4 + 1 < 4:
                                nxt = load_pa(c4 + 1)
                            if c4 == 2:
                                stT = ExitStack()
                                gmlp = sb(stT, "gmlp", [128, D], F32, top=True); b_gmlp = S.buf("gmlp")
                                S.dma("sp", gmlp[:], nmlp_d.to_broadcast([128, D]), writes=[b_gmlp])
                                wo = sb(stT, "wo", [128, 8, D], BF16, top=True); b_wo = S.buf("wo")
                                S.dma("pool", wo[:, 0:4, :], w_out.rearrange("(kc p) o -> p kc o", p=128)[:, 0:4, :], writes=[b_wo])
                                S.dma("pool", wo[:, 4:8, :], w_out.rearrange("(kc p) o -> p kc o", p=128)[:, 4:8, :], writes=[b_wo])
                            for f2 in range(2):
                                fc = c4 * 2 + f2
                                for t4 in range(4):
                                    pg = pj.next()
                                    proj_fm(pg, wga, slice(f2 * 128, (f2 + 1) * 128), 128, t4, bwga)
                                    sg_, bsg = sg_ring.next()
                                    S.op("act", lambda e: e.activation(out=sg_[:], in_=PS[pg][:, :], func=AF.Sigmoid),
                                         reads=[bPS[pg]], writes=[bsg])
                                    pp = pj.next()
                                    for i in range(12):
                                        S.op("pe", lambda e: e.matmul(PS[pp][:, :], lhsT=wp[:, i, f2 * 128:(f2 + 1) * 128],
                                                                      rhs=yaT[:, i, t4 * 512:(t4 + 1) * 512],
                                                                      start=(i == 0), stop=(i == 11)),
                                             reads=[bwp, b_yaT[i][t4]], writes=[bPS[pp]])
                                    tm, btm = tmp_ring.next()
                                    S.op("dve", lambda e: e.tensor_tensor(out=tm[:], in0=PS[pp][:, :], in1=sg_[:], op=ALU.mult),
                                         reads=[bPS[pp], bsg], writes=[btm])
                                    S.op("pool", lambda e: e.tensor_tensor(out=mgT[:, fc, t4 * 512:(t4 + 1) * 512],
                                                                           in0=mgT[:, fc, t4 * 512:(t4 + 1) * 512], in1=tm[:],
                                                                           op=ALU.add),
                                         reads=[btm, b_mgT[t4]], writes=[b_mgT[t4]])
                        S.barrier()
                if debug:
                    S.dma("sp", dbg["d_mgT"], mgT[:], reads=b_mgT, own=b_mgT[0])
                    S.barrier()
                sx.close()
                if stop_after == "rglru":
                    stT.close()
                    continue

                with stT as st2:
                    HT_ = 1024
                    hnT = sb(st2, "hnT", [128, 8, HT_], BF16); b_hnT = [S.buf("hnT") for _ in range(8)]
                    AT = sb(st2, "AT", [128, 32, HT_], BF16); b_AT = [[S.buf("AT") for _ in range(2)] for _ in range(32)]
                    wmi = [sb(st2, f"wmi{i}", [128, 8, 256], BF16) for i in range(2)]
                    wmi_ring = Ring([(wmi[i], S.buf("wmi")) for i in range(2)])
                    wmo = [sb(st2, f"wmo{i}", [128, 32, 256], BF16) for i in range(2)]
                    wmo_ring = Ring([(wmo[i], S.buf("wmo")) for i in range(2)])
                    xts = [sb(st2, f"xt2{i}", [128, D], F32) for i in range(2)]
                    xring = Ring([(xts[i], S.buf("xt2")) for i in range(2)])
                    hts = [sb(st2, f"ht{i}", [128, D], F32) for i in range(2)]
                    hring = Ring([(hts[i], S.buf("ht")) for i in range(2)])
                    hnb = [sb(st2, f"hnb{i}", [128, D], BF16) for i in range(2)]
                    hnring = Ring([(hnb[i], S.buf("hnb")) for i in range(2)])
                    rl = [sb(st2, f"rl{i}", [128, 512], F32) for i in range(2)]
                    rl_ring = Ring([(rl[i], S.buf("rl")) for i in range(2)])
                    xq = [sb(st2, f"xq{i}", [128, 256], F32) for i in range(2)]
                    xq_ring = Ring([(xq[i], S.buf("xq")) for i in range(2)])
                    oq = [sb(st2, f"oq{i}", [128, 256], F32) for i in range(2)]
                    oq_ring = Ring([(oq[i], S.buf("oq")) for i in range(2)])
                    pj = Ring([0, 1, 2, 3, 4, 5])
                    wmi_v = w_mi.rearrange("(kc p) f -> p kc f", p=128)
                    wmo_v = w_mo.rearrange("(fc p) o -> p fc o", p=128)
                    for hh in range(2):
                        tok0 = hh * HT_
                        def hA(tl):
                            tt = hh * 8 + tl
                            xt, bxt = xring.next()
                            S.dma("sp", xt[:], x[s, tt * 128:(tt + 1) * 128, :], writes=[bxt])
                            ht, bht = hring.next()
                            for ch in range(2):
                                pi = pj.next()
                                for kc in range(8):
                                    S.op("pe", lambda e: e.matmul(PS[pi][:, :], lhsT=mgT[:, kc, tt * 128:(tt + 1) * 128],
                                                                  rhs=wo[:, kc, ch * 512:(ch + 1) * 512],
                                                                  start=(kc == 0), stop=(kc == 7)),
                                         reads=[b_wo, b_mgT[tt // 4]], writes=[bPS[pi]])
                                S.op("dve", lambda e: e.tensor_tensor(out=ht[:, ch * 512:(ch + 1) * 512], in0=PS[pi][:, :],
                                                                      in1=xt[:, ch * 512:(ch + 1) * 512], op=ALU.add),
                                     reads=[bPS[pi], bxt], writes=[bht])
                            stt, bst = stat_ring.next()
                            hb, bhb = hnring.next()
                            S.op("act", lambda e: e.activation(out=hb[:], in_=ht[:], func=AF.Square,
                                                               accum_out=stt[:, 0:1]), reads=[bht], writes=[bhb, bst])
                            S.op("act", lambda e: e.activation(out=stt[:, 1:2], in_=stt[:, 0:1], func=AF.Sqrt,
                                                               scale=1.0 / D, bias=EPS), reads=[bst], writes=[bst])
                            S.op("dve", lambda e: e.reciprocal(out=stt[:, 2:3], in_=stt[:, 1:2]), reads=[bst], writes=[bst])
                            S.op("dve", lambda e: e.scalar_tensor_tensor(out=hb[:], in0=ht[:], scalar=stt[:, 2:3],
                                                                         in1=gmlp[:], op0=ALU.mult, op1=ALU.mult),
                                 reads=[bht, bst, b_gmlp], writes=[bhb])
                            return hb, bhb

                        def hB(tl, hb, bhb):
                            ti = 6 + (tl % 2)
                            for kc in range(8):
                                S.op("pe", lambda e: e.transpose(out=psb(ti)[:, kc * 128:(kc + 1) * 128],
                                                                 in_=hb[:, kc * 128:(kc + 1) * 128], identity=ident[:]),
                                     reads=[bhb, b_ident], writes=[bPS[ti]])
                            S.op("act", lambda e: e.copy(out=hnT[:, :, tl * 128:(tl + 1) * 128],
                                                         in_=psb(ti)[:, 0:1024].rearrange("p (a b) -> p a b", a=8)),
                                 reads=[bPS[ti]], writes=[b_hnT[tl]])

                        prev = None
                        for tl in range(8):
                            cur = hA(tl)
                            if prev is not None:
                                hB(tl - 1, *prev)
                            prev = cur
                        hB(7, *prev)
                        def load_wmi(f8):
                            wm, bwm = wmi_ring.next()
                            S.dma("pool", wm[:], wmi_v[:, :, f8 * 256:(f8 + 1) * 256], writes=[bwm])
                            return wm, bwm

                        def load_wmo(cq):
                            wq, bwq = wmo_ring.next()
                            S.dma("pool", wq[:, 0:16, :], wmo_v[:, 0:16, cq * 256:(cq + 1) * 256], writes=[bwq])
                            S.dma("pool", wq[:, 16:32, :], wmo_v[:, 16:32, cq * 256:(cq + 1) * 256], writes=[bwq])
                            return wq, bwq

                        nxt_wm = load_wmi(0)
                        for f8 in range(16):
                            wm, bwm = nxt_wm
                            if f8 + 1 < 16:
                                nxt_wm = load_wmi(f8 + 1)
                            else:
                                nxt_wq = load_wmo(0)
                            for f2 in range(2):
                                ffc = f8 * 2 + f2
                                for t4 in range(2):
                                    pi = pj.next()
                                    for kc in range(8):
                                        S.op("pe", lambda e: e.matmul(PS[pi][:, :], lhsT=wm[:, kc, f2 * 128:(f2 + 1) * 128],
                                                                      rhs=hnT[:, kc, t4 * 512:(t4 + 1) * 512],
                                                                      start=(kc == 0), stop=(kc == 7)),
                                             reads=[bwm] + b_hnT[t4 * 4:(t4 + 1) * 4], writes=[bPS[pi]])
                                    r_, br = rl_ring.next()
                                    S.op("act", lambda e: e.activation(out=r_[:], in_=PS[pi][:, :], func=AF.Relu),
                                         reads=[bPS[pi]], writes=[br])
                                    S.op("dve", lambda e: e.tensor_tensor(out=AT[:, ffc, t4 * 512:(t4 + 1) * 512],
                                                                          in0=r_[:], in1=r_[:], op=ALU.mult),
                                         reads=[br], writes=[b_AT[ffc][t4]])
                        for cq in range(4):
                            wq, bwq = nxt_wq
                            if cq + 1 < 4:
                                nxt_wq = load_wmo(cq + 1)
                            for tl in range(8):
                                tt = hh * 8 + tl
                                xq_, bxq = xq_ring.next()
                                S.dma("sp", xq_[:], x[s, tt * 128:(tt + 1) * 128, cq * 256:(cq + 1) * 256], writes=[bxq])
                                pi = pj.next()
                                for kc in range(8):
                                    S.op("pe", lambda e: e.matmul(PS[pi][:, 0:256], lhsT=mgT[:, kc, tt * 128:(tt + 1) * 128],
                                                                  rhs=wo[:, kc, cq * 256:(cq + 1) * 256],
                                                                  start=(kc == 0), stop=False),
                                         reads=[b_wo, b_mgT[tt // 4]], writes=[bPS[pi]])
                                for ffc in range(32):
                                    S.op("pe", lambda e: e.matmul(PS[pi][:, 0:256], lhsT=AT[:, ffc, tl * 128:(tl + 1) * 128],
                                                                  rhs=wq[:, ffc, :], start=False, stop=(ffc == 31)),
                                         reads=[bwq, b_AT[ffc][tl // 4]], writes=[bPS[pi]])
                                oq_, boq = oq_ring.next()
                                S.op("dve", lambda e: e.tensor_tensor(out=oq_[:], in0=PS[pi][:, 0:256], in1=xq_[:], op=ALU.add),
                                     reads=[bPS[pi], bxq], writes=[boq])
                                S.dma("sp", out[s, tt * 128:(tt + 1) * 128, cq * 256:(cq + 1) * 256], oq_[:], reads=[boq])
                    S.barrier()
        S.barrier(["sp"])
        print(f"[kernel] instructions ~{S.ninst}, counts {S.cnt}, sbuf peak {peak[0]}")
    return nc


def prep_shared(inp):
    f = lambda a: np.ascontiguousarray(np.asarray(a, dtype=np.float32))
    sh = {}
    sh["w_in"] = f(inp["w_in"][0])
    cw = inp["conv_w"][0]
    tab = np.stack([cw[0], cw[1], cw[2], cw[3], inp["conv_b"][0], inp["gate_a_b"][0].reshape(-1),
                    inp["gate_x_b"][0].reshape(-1), inp["lru_lambda"][0]], axis=-1)
    sh["chtab"] = f(tab.reshape(12, 112, 8).transpose(1, 0, 2))
    sh["gate_a_w"] = f(inp["gate_a_w"][0])
    sh["gate_x_w"] = f(inp["gate_x_w"][0])
    w1k = inp["phi_k_w1"][0].reshape(32, 64, 256).transpose(1, 0, 2)
    w1v = inp["phi_v_w1"][0].reshape(32, 64, 256).transpose(1, 0, 2)
    sh["w1kv"] = f(np.concatenate([w1k, w1v], axis=0))
    sh["peT"] = f(np.concatenate([inp["phi_k_pe"][0].T, inp["phi_v_pe"][0].T], axis=0))
    w2 = np.stack([inp["phi_k_w2"][0].reshape(2, 128, 64), inp["phi_v_w2"][0].reshape(2, 128, 64)], axis=0)
    sh["w2kv"] = f(w2.transpose(2, 0, 1, 3))
    sh["gains"] = f(np.stack([inp["q_norm"][0], inp["kc_norm"][0], inp["ks_norm"][0], inp["kw_norm"][0]], axis=-1))
    sh["rel_bias"] = f(inp["rel_bias"])
    for k in ("proj_a", "proj_b", "w_out", "w_mlp_in", "w_mlp_out"):
        sh[k] = f(inp[k][0])
    sh["norm_mix"] = f(inp["norm_mix"][0].reshape(1, D))
    sh["norm_mlp"] = f(inp["norm_mlp"][0].reshape(1, D))
    sh.update(host_consts())
    return sh


_PROG = {}


def kernel(**inputs):
    n = 8
    sh = prep_shared(inputs)
    x = np.asarray(inputs["x"], dtype=np.float32)
    if "main" not in _PROG:
        _PROG["main"] = build_program(nseq=2)
    nc = _PROG["main"]
    in_maps = []
    for c in range(n):
        m = dict(sh)
        m["x"] = np.ascontiguousarray(x[2 * c:2 * c + 2])
        in_maps.append(m)
    res = run_bass_kernel_spmd(nc, in_maps, core_ids=list(range(n)))
    return np.concatenate([r["out"] for r in res.results], axis=0).astype(np.float32)
```
